# Optimizing a Trainium2 kernel written in Bass

```python
import jax, jax.numpy as jnp
from jax import lax
import numpy as np

D_MODEL = 2048
BATCH = 16
SEQ = 256
DEPTH = 1
DEC_BATCH = 2
DEC_SEQ = 2048
PAST_LEN = 256

GRID_W = 64
D_FF = 5632
CONV_W = 2048
CONV_K = 3
SSM_INNER = 2048
SSM_HEADDIM = 64
SSM_HEADS = SSM_INNER // SSM_HEADDIM
SSM_GROUPS = 4
SSM_STATE = 128
SSM_CONV_K = 3
CHUNK = 128
N_MOD = 9
EPS = 1e-6
XBC_W = SSM_INNER + 2 * SSM_GROUPS * SSM_STATE
IN_SPLITS = (CONV_W, CONV_W, CONV_W, SSM_INNER, XBC_W, SSM_HEADS, SSM_HEADS, D_MODEL, D_MODEL)
IN_COLS = 3 * CONV_W + SSM_INNER + XBC_W + 2 * SSM_HEADS + 2 * D_MODEL

kernel_name = "hybrid_conv_ssd_macaron_dit_step"


def rmsnorm(x, w):
    xf = x.astype(jnp.float32)
    y = xf * lax.rsqrt(jnp.mean(xf * xf, axis=-1, keepdims=True) + EPS)
    return (y * w.astype(jnp.float32)).astype(x.dtype)


def modulate(x, shift, scale):
    return x * (1 + scale) + shift


def swiglu(x, w_gate, w_up, w_down):
    return (jax.nn.silu(x @ w_gate) * (x @ w_up)) @ w_down


def dwconv_centred(u, w):
    k = w.shape[0]
    return lax.conv_general_dilated(u, w[:, None, :].astype(u.dtype), window_strides=(1,),
                                    padding=[(k // 2, k // 2)],
                                    dimension_numbers=('NWC', 'WIO', 'NWC'),
                                    feature_group_count=u.shape[-1])


def split_in(proj):
    idx, acc = [], 0
    for s in IN_SPLITS[:-1]:
        acc += s
        idx.append(acc)
    return jnp.split(proj, idx, axis=-1)


def segsum(a):
    t = a.shape[-1]
    ar = jnp.broadcast_to(a[..., :, None], a.shape + (t,))
    ar = jnp.where(jnp.tril(jnp.ones((t, t), bool), -1), ar, 0.0)
    s = jnp.cumsum(ar, axis=-2)
    return jnp.where(jnp.tril(jnp.ones((t, t), bool), 0), s, -jnp.inf)


def ssd_scan(x, dt, A, B, C, h0):
    f32 = jnp.float32
    b, l, h, p = x.shape
    g, n = B.shape[-2:]
    r = h // g
    c = l // CHUNK
    xd = (x.astype(f32) * dt[..., None]).reshape(b, c, CHUNK, g, r, p)
    a = jnp.transpose((dt * A).reshape(b, c, CHUNK, g, r), (0, 3, 4, 1, 2))
    a_cs = jnp.cumsum(a, axis=-1)
    Bc = B.astype(f32).reshape(b, c, CHUNK, g, n)
    Cc = C.astype(f32).reshape(b, c, CHUNK, g, n)
    Lmat = jnp.exp(segsum(a))
    CB = jnp.einsum('bclgn,bcsgn->bcgls', Cc, Bc)
    y_diag = jnp.einsum('bcgls,bgrcls,bcsgrp->bclgrp', CB, Lmat, xd)
    decay_states = jnp.exp(a_cs[..., -1:] - a_cs)
    states = jnp.einsum('bclgn,bgrcl,bclgrp->bcgrpn', Bc, decay_states, xd)
    states = jnp.concatenate([h0.astype(f32).reshape(b, 1, g, r, p, n), states], axis=1)
    chunk_tot = jnp.pad(a_cs[..., -1], ((0, 0), (0, 0), (0, 0), (1, 0)))
    decay_chunk = jnp.exp(segsum(chunk_tot))
    states = jnp.einsum('bgrzc,bcgrpn->bzgrpn', decay_chunk, states)
    prev, final = states[:, :-1], states[:, -1]
    y_off = jnp.einsum('bclgn,bcgrpn,bgrcl->bclgrp', Cc, prev, jnp.exp(a_cs))
    y = (y_diag + y_off).reshape(b, l, h, p)
    return y, final.reshape(b, h, p, n)


def mamba2_bidir(z, xbc, dtf_raw, dtb_raw, h0_f, h0_b, ssm_conv_w, ssm_conv_b, dt_bias_f, dt_bias_b,
                 a_log_f, a_log_b, d_skip, ssm_norm_w):
    b, l, _ = xbc.shape
    f32 = jnp.float32
    xbc = jax.nn.silu(dwconv_centred(xbc, ssm_conv_w) + ssm_conv_b)
    xs, Bs, Cs = jnp.split(xbc, [SSM_INNER, SSM_INNER + SSM_GROUPS * SSM_STATE], axis=-1)
    x = xs.reshape(b, l, SSM_HEADS, SSM_HEADDIM)
    B = Bs.reshape(b, l, SSM_GROUPS, SSM_STATE)
    C = Cs.reshape(b, l, SSM_GROUPS, SSM_STATE)
    dt_f = jax.nn.softplus(dtf_raw.astype(f32) + dt_bias_f.astype(f32))
    dt_b = jax.nn.softplus(dtb_raw.astype(f32) + dt_bias_b.astype(f32))
    A_f = -jnp.exp(a_log_f.astype(f32))
    A_b = -jnp.exp(a_log_b.astype(f32))
    y_f, hf = ssd_scan(x, dt_f, A_f, B, C, h0_f)
    flip = lambda t: jnp.flip(t, axis=1)
    y_b, hb = ssd_scan(flip(x), flip(dt_b), A_b, flip(B), flip(C), h0_b)
    y = y_f + flip(y_b) + x.astype(f32) * d_skip.astype(f32)[:, None]
    y = y.reshape(b, l, SSM_INNER).astype(z.dtype)
    y = rmsnorm(y * jax.nn.silu(z), ssm_norm_w)
    return y, hf, hb


def trunk_layer(h, cvec, row_len, h0_f, h0_b, w_ada, b_ada, norm1_w, ffn1_w_gate, ffn1_w_up, ffn1_w_down,
                norm2_w, w_in, conv_w, ssm_conv_w, ssm_conv_b, dt_bias_f, dt_bias_b, a_log_f, a_log_b,
                d_skip, ssm_norm_w, w_conv_out, w_ssm_out, w_o, norm3_w, ffn2_w_gate, ffn2_w_up, ffn2_w_down):
    b, L, _ = h.shape
    mod = (jax.nn.silu(cvec) @ w_ada + b_ada)[:, None, :]
    sh1, sc1, g1, sh2, sc2, g2, sh3, sc3, g3 = jnp.split(mod, N_MOD, axis=-1)
    u = modulate(rmsnorm(h, norm1_w), sh1, sc1)
    h = h + 0.5 * g1 * swiglu(u, ffn1_w_gate, ffn1_w_up, ffn1_w_down)
    u = modulate(rmsnorm(h, norm2_w), sh2, sc2)
    cb, cc, cx, z, xbc, dtf, dtb, gc, gs = split_in(u @ w_in)
    v = (cc * cx).reshape(b * (L // row_len), row_len, CONV_W)
    v = dwconv_centred(v, conv_w).reshape(b, L, CONV_W)
    y_conv = (cb * v) @ w_conv_out
    y_ssm, hf, hb = mamba2_bidir(z, xbc, dtf, dtb, h0_f, h0_b, ssm_conv_w, ssm_conv_b, dt_bias_f, dt_bias_b,
                                 a_log_f, a_log_b, d_skip, ssm_norm_w)
    y_ssm = y_ssm @ w_ssm_out
    merged = jax.nn.sigmoid(gc) * y_conv + jax.nn.sigmoid(gs) * y_ssm
    h = h + g2 * (merged @ w_o)
    u = modulate(rmsnorm(h, norm3_w), sh3, sc3)
    h = h + 0.5 * g3 * swiglu(u, ffn2_w_gate, ffn2_w_up, ffn2_w_down)
    return h, hf, hb


def setup_inputs(seed: int = 0) -> dict:
    key = jax.random.key(seed)
    ks = iter(jax.random.split(key, 40))
    nrm = lambda shape, s: jax.random.normal(next(ks), shape, jnp.float32) * s
    D = D_MODEL
    gain = lambda shape: 1.0 + nrm(shape, 0.02)
    dt0 = jnp.exp(jax.random.uniform(next(ks), (2, DEPTH, SSM_HEADS), jnp.float32,
                                     float(np.log(1e-3)), float(np.log(1e-1))))
    dt_bias = dt0 + jnp.log(-jnp.expm1(-dt0))
    a_log = jnp.log(jax.random.uniform(next(ks), (2, DEPTH, SSM_HEADS), jnp.float32, 1.0, 16.0))
    st_shape = (DEC_BATCH, DEPTH, SSM_HEADS, SSM_HEADDIM, SSM_STATE)
    return {
        "x_prompt": nrm((BATCH, SEQ, D), 1.0),
        "x_sample": nrm((DEC_BATCH, DEC_SEQ, D), 1.0),
        "c": nrm((DEC_BATCH, D), 1.0),
        "state_ssm_fwd": nrm(st_shape, 0.5),
        "state_ssm_bwd": nrm(st_shape, 0.5),
        "c_ctx": nrm((D,), 1.0),
        "w_ada": nrm((DEPTH, D, N_MOD * D), 0.5 * D ** -0.5),
        "b_ada": nrm((DEPTH, N_MOD * D), 0.02),
        "norm1_w": gain((DEPTH, D)),
        "ffn1_w_gate": nrm((DEPTH, D, D_FF), D ** -0.5),
        "ffn1_w_up": nrm((DEPTH, D, D_FF), D ** -0.5),
        "ffn1_w_down": nrm((DEPTH, D_FF, D), D_FF ** -0.5),
        "norm2_w": gain((DEPTH, D)),
        "w_in": nrm((DEPTH, D, IN_COLS), D ** -0.5),
        "conv_w": nrm((DEPTH, CONV_K, CONV_W), CONV_K ** -0.5),
        "ssm_conv_w": nrm((DEPTH, SSM_CONV_K, XBC_W), SSM_CONV_K ** -0.5),
        "ssm_conv_b": nrm((DEPTH, XBC_W), 0.02),
        "dt_bias_f": dt_bias[0],
        "dt_bias_b": dt_bias[1],
        "a_log_f": a_log[0],
        "a_log_b": a_log[1],
        "d_skip": gain((DEPTH, SSM_HEADS)),
        "ssm_norm_w": gain((DEPTH, SSM_INNER)),
        "w_conv_out": nrm((DEPTH, CONV_W, D), CONV_W ** -0.5),
        "w_ssm_out": nrm((DEPTH, SSM_INNER, D), SSM_INNER ** -0.5),
        "w_o": nrm((DEPTH, D, D), D ** -0.5),
        "norm3_w": gain((DEPTH, D)),
        "ffn2_w_gate": nrm((DEPTH, D, D_FF), D ** -0.5),
        "ffn2_w_up": nrm((DEPTH, D, D_FF), D ** -0.5),
        "ffn2_w_down": nrm((DEPTH, D_FF, D), D_FF ** -0.5),
        "final_norm_w": gain((D,)),
    }


def reference(x_prompt, x_sample, c, state_ssm_fwd, state_ssm_bwd, c_ctx, w_ada, b_ada, norm1_w,
              ffn1_w_gate, ffn1_w_up, ffn1_w_down, norm2_w, w_in, conv_w, ssm_conv_w, ssm_conv_b,
              dt_bias_f, dt_bias_b, a_log_f, a_log_b, d_skip, ssm_norm_w, w_conv_out, w_ssm_out, w_o,
              norm3_w, ffn2_w_gate, ffn2_w_up, ffn2_w_down, final_norm_w):
    n_ctx = x_prompt.shape[1]
    rows = x_sample.shape[1] // GRID_W
    zero_state = jnp.zeros((x_prompt.shape[0], SSM_HEADS, SSM_HEADDIM, SSM_STATE), x_prompt.dtype)
    hp, hs = x_prompt, x_sample
    new_f, new_b = [], []
    for l in range(DEPTH):
        params = (w_ada[l], b_ada[l], norm1_w[l], ffn1_w_gate[l], ffn1_w_up[l], ffn1_w_down[l],
                  norm2_w[l], w_in[l], conv_w[l], ssm_conv_w[l], ssm_conv_b[l], dt_bias_f[l], dt_bias_b[l],
                  a_log_f[l], a_log_b[l], d_skip[l], ssm_norm_w[l], w_conv_out[l], w_ssm_out[l], w_o[l],
                  norm3_w[l], ffn2_w_gate[l], ffn2_w_up[l], ffn2_w_down[l])
        hp, hf, hb = trunk_layer(hp, c_ctx[None, :], n_ctx, zero_state, zero_state, *params)
        new_f.append(hf.astype(x_prompt.dtype))
        new_b.append(hb.astype(x_prompt.dtype))
        hs, _, _ = trunk_layer(hs, c, rows and GRID_W, state_ssm_fwd[:, l], state_ssm_bwd[:, l], *params)
    y_prompt = rmsnorm(hp, final_norm_w)
    y_sample = rmsnorm(hs, final_norm_w)
    new_state_ssm_fwd = jnp.stack(new_f, axis=1)
    new_state_ssm_bwd = jnp.stack(new_b, axis=1)
    return (y_prompt, y_sample, new_state_ssm_fwd, new_state_ssm_bwd)
```

```python
import numpy as np
from contextlib import ExitStack
import concourse.bass as bass
import concourse.mybir as mybir
from concourse.bass_utils import run_bass_kernel_spmd

F32 = mybir.dt.float32
F32R = mybir.dt.float32r
AF = mybir.ActivationFunctionType
ALU = mybir.AluOpType

D = 2048
DFF = 5632
T = 512
NT = 4
NCK = 16
EPS = 1e-6
C_CB, C_CC, C_CX, C_Z, C_XBC, C_DT, C_GC, C_GS = 0, 2048, 4096, 6144, 8192, 11264, 11328, 13376
NSLOT = 4
NMOD1 = 80
SUB = 99
VERBOSE = False

P_BADA = 0
P_N1 = P_BADA + 144
P_N2 = P_N1 + 16
P_N3 = P_N2 + 16
P_SN = P_N3 + 16
P_CW = P_SN + 16
P_SCW = P_CW + 48
P_SCB = P_SCW + 72
P_DTB = P_SCB + 24
P_ALOG = P_DTB + 64
P_DSK = P_ALOG + 64
P_MSK = P_DSK + 32
NPAR = P_MSK + 24


class Buf:
    __slots__ = ("lw", "rd")

    def __init__(self):
        self.lw = None
        self.rd = {}


class Sig:
    def __init__(self, sem, step):
        self.sem = sem
        self.step = step
        self.count = 0
        self.is_dma = step == 16


class Eng:
    def __init__(self, h, sig, skip_self=False):
        self.h = h
        self.sig = sig
        self.seen = {}
        self.skip_self = skip_self


class Tl:
    def __init__(self, t, nb):
        self.t = t
        self.b = [Buf() for _ in range(nb)]


class K:
    def __init__(self, nc, es):
        self.nc = nc
        self.es = es
        self.nsem = 0
        self.pe = Eng(nc.tensor, self.new_sig(1), True)
        self.act = Eng(nc.scalar, self.new_sig(1))
        self.dve = Eng(nc.vector, self.new_sig(1))
        self.pool = Eng(nc.gpsimd, self.new_sig(1))
        self.sp = Eng(nc.sync, self.new_sig(1))
        self.engs = [self.pe, self.act, self.dve, self.pool, self.sp]
        self.lanes = []
        self.nm = 0
        self.sp_lanes = [self.lane() for _ in range(8)]
        self.sp_i = 0

    def new_sig(self, step):
        sem = self.es.enter_context(self.nc.semaphore(f"sem{self.nsem}"))
        self.nsem += 1
        return Sig(sem, step)

    def lane(self):
        s = self.new_sig(16)
        self.lanes.append(s)
        return s

    def sb(self, es, shape, nb=1, dtype=F32):
        self.nm += 1
        return Tl(es.enter_context(self.nc.sbuf_tensor(f"t{self.nm}", list(shape), dtype)), nb)

    def op(self, eng, fn, reads=(), writes=(), sig=None, inc=True):
        deps = {}
        for b in reads:
            if b.lw is not None and deps.get(b.lw[0], 0) < b.lw[1]:
                deps[b.lw[0]] = b.lw[1]
        for b in writes:
            if b.lw is not None and deps.get(b.lw[0], 0) < b.lw[1]:
                deps[b.lw[0]] = b.lw[1]
            for s, v in b.rd.items():
                if deps.get(s, 0) < v:
                    deps[s] = v
        for s, v in deps.items():
            if s.is_dma:
                v = s.count
            if s is eng.sig and eng.skip_self:
                continue
            if eng.seen.get(s, 0) >= v:
                continue
            eng.h.wait_ge(s.sem, v)
            eng.seen[s] = v
        ins = fn(eng.h)
        sig = sig or eng.sig
        if inc:
            ins.then_inc(sig.sem, sig.step)
            sig.count += sig.step
            tick = sig.count
        else:
            tick = sig.count + sig.step
        for b in writes:
            b.lw = (sig, tick)
            b.rd = {}
        for b in reads:
            if b.rd.get(sig, 0) < tick:
                b.rd[sig] = tick
        return ins

    def spdma(self, fn, reads, writes):
        lane = self.sp_lanes[self.sp_i % len(self.sp_lanes)]
        self.sp_i += 1
        if lane.count > self.sp.seen.get(lane, 0):
            self.sp.h.wait_ge(lane.sem, lane.count)
            self.sp.seen[lane] = lane.count
        return self.op(self.sp, fn, reads, writes, sig=lane)

    def barrier(self, only=None):
        sigs = [e.sig for e in self.engs] + self.lanes
        for e in (only or (self.act, self.dve, self.sp)):
            for s in sigs:
                if s is e.sig:
                    continue
                if s.count > e.seen.get(s, 0):
                    e.h.wait_ge(s.sem, s.count)
                    e.seen[s] = s.count


def build(debug=(), stage=99):
    nc = bass.Bass("TRN2", target_bir_lowering=False)

    def din(name, shape):
        return nc.dram_tensor(name, list(shape), F32, kind="ExternalInput").ap()

    def dout(name, shape):
        return nc.dram_tensor(name, list(shape), F32, kind="ExternalOutput").ap()

    xs = din("xs", [5, T, D])
    cvd = din("cv", [128, 32])
    h0d = din("h0", [2, 2048, 128])
    pard = din("par", [128, NPAR])
    cstd = din("cst", [128, 6 * 128])
    fnwd = din("fnw", [D])
    w_ada = din("w_ada", [144, 128, 2048])
    f1g, f1u, f1d = din("f1g", [44, 128, 2048]), din("f1u", [44, 128, 2048]), din("f1d", [DFF, D])
    f2g, f2u, f2d = din("f2g", [44, 128, 2048]), din("f2u", [44, 128, 2048]), din("f2d", [DFF, D])
    w_in = din("w_in", [120, 128, 2048])
    w_dt = din("w_dt", [128, 1024])
    wco, wso, wo = din("wco", [16, 128, 2048]), din("wso", [D, D]), din("wo", [16, 128, 2048])
    y_s, y_p = dout("y_s", [T, D]), dout("y_p", [T, D])
    st_f, st_b = dout("st_f", [2, 2048, 128]), dout("st_b", [2, 2048, 128])
    scr = nc.dram_tensor("scr", [5, 40, 128, T], F32, kind="Internal").ap()
    dbg_out = {}

    with ExitStack() as es:
        k = K(nc, es)
        pe, act, dve, pool, sp = k.pe, k.act, k.dve, k.pool, k.sp

        CST = k.sb(es, [128, 6, 128])
        PAR = k.sb(es, [128, NPAR])
        MODD = k.sb(es, [128, 2, 9, 16])
        SCV = k.sb(es, [128, 16, 2])
        H0 = k.sb(es, [128, NCK, T], NCK)
        DT = k.sb(es, [128, 5, NT, 64], 5)
        AA = k.sb(es, [128, 5, NT, 64], 5)
        ANEG = k.sb(es, [128, 64])
        EDGE = k.sb(es, [128, 4, 24, 2], 4)
        HF = k.sb(es, [128, 4, T], 4)
        HB = k.sb(es, [128, 4, T], 4)
        banks = [Tl(es.enter_context(nc.psum_tensor(f"bank{i}", [128, 512], F32)), 1) for i in range(8)]
        slots = [k.sb(es, [128, 2048]) for _ in range(NSLOT)]
        slanes = [k.lane() for _ in range(NSLOT)]
        st = {"bank": 0, "slot": 0}
        ident, ones = CST.t[:, 0, :], CST.t[:, 1, :]
        Tle, Tgt, Tge, Tlt = (CST.t[:, i, :] for i in (2, 3, 4, 5))
        cB = CST.b[0]
        pB = PAR.b[0]

        def bank():
            i = st["bank"]
            st["bank"] = (i + 1) % 7
            return banks[i]
        stat_bank = banks[7]

        def R(ap):
            return ap.bitcast(F32R)

        def wload(src, pat=None, **kw):
            i = st["slot"] % len(slots)
            st["slot"] = (i + 1) % len(slots)
            if i >= NSLOT and st.get("fence"):
                k.barrier(only=(pool,))
                st["fence"] = False
            s = slots[i]
            n = 1
            for d_ in src.shape[1:]:
                n *= d_
            v = s.t[:, 0:n]
            if pat is not None:
                v = v.rearrange(pat, **kw)
            k.op(pool, lambda h: h.dma_start(out=R(v), in_=src), [], [s.b[0]], sig=slanes[i])
            return v, s.b[0]

        class extra_slots:
            def __init__(self, ph, n):
                st["fence"] = True
                st["slot"] = 0
                n = min(n, nc.sbuf_bytes_remaining // 8192)
                for _ in range(n):
                    slots.append(k.sb(ph, [128, 2048]))
                    slanes.append(k.lane())
                self.n = n

            def close(self):
                for _ in range(self.n):
                    slots.pop()
                    slanes.pop()
                st["slot"] = 0
                st["fence"] = False

        def colpanel(w, c0):
            if w is w_in:
                j = c0 // 128 if c0 < C_DT else 88 + (c0 - C_GC) // 128
            else:
                j = c0 // 128
            return w[j]

        def mm(out, lhsT, rhs, start, stop, reads, writes, r=True, inc=None):
            if r:
                lhsT, rhs = R(lhsT), R(rhs)
            return k.op(pe, lambda h: h.matmul(out, lhsT=lhsT, rhs=rhs, start=start, stop=stop),
                        reads, writes, inc=(stop if inc is None else inc))

        def tap(name, tl_ap, bufs):
            if name in debug:
                shp = list(tl_ap.shape)
                o = dout("dbg_" + name, shp)
                dbg_out[name] = shp
                k.spdma(lambda h: h.dma_start(out=o, in_=tl_ap), bufs, [Buf()])

        k.op(pool, lambda h: h.dma_start(out=R(CST.t[:].rearrange("p a b -> p (a b)")), in_=cstd), [], [cB], sig=k.lane())
        k.spdma(lambda h: h.dma_start(out=PAR.t[:], in_=pard), [], [pB])
        k.op(act, lambda h: h.activation(out=ANEG.t[:], in_=PAR.t[:, P_ALOG:P_ALOG + 64], func=AF.Exp), [pB], ANEG.b)
        k.op(dve, lambda h: h.tensor_scalar(out=ANEG.t[:], in0=ANEG.t[:], scalar1=-1.0, scalar2=None, op0=ALU.mult),
             ANEG.b, ANEG.b)

        with ExitStack() as ph:
            CV = k.sb(ph, [128, 16, 2])
            MOD = k.sb(ph, [128, NMOD1, 2])
            k.spdma(lambda h: h.dma_start(out=CV.t[:].rearrange("p a b -> p (a b)"), in_=cvd), [], CV.b)
            k.op(act, lambda h: h.activation(out=R(SCV.t[:]), in_=CV.t[:], func=AF.Silu), CV.b, SCV.b)
            mb = bank()
            for j in range(NMOD1):
                wv, wb = wload(colpanel(w_ada, j * 128), "p (k c) -> p k c", c=128)
                for kc in range(NCK):
                    mm(mb.t[:, 2 * j:2 * j + 2], wv[:, kc, :], SCV.t[:, kc, :], kc == 0, kc == NCK - 1,
                       [wb, SCV.b[0]], mb.b)
            k.op(dve, lambda h: h.tensor_tensor(
                out=MOD.t[:], in0=mb.t[:, 0:2 * NMOD1].rearrange("p (j t) -> p j t", t=2),
                in1=PAR.t[:, P_BADA:P_BADA + NMOD1].unsqueeze(2).to_broadcast([128, NMOD1, 2]), op=ALU.add),
                [mb.b[0], pB], MOD.b)
            for t_ in range(2):
                for n_, pn in enumerate((P_N1, P_N2)):
                    base = 48 * n_
                    k.op(dve, lambda h: h.scalar_tensor_tensor(
                        out=MODD.t[:, t_, 3 * n_, :], in0=MOD.t[:, base + 16:base + 32, t_], scalar=1.0,
                        in1=PAR.t[:, pn:pn + 16], op0=ALU.add, op1=ALU.mult), [MOD.b[0], pB], MODD.b)
                    k.op(dve, lambda h: h.tensor_copy(out=MODD.t[:, t_, 3 * n_ + 1, :], in_=MOD.t[:, base:base + 16, t_]),
                         MOD.b, MODD.b)
                    if n_ == 0:
                        k.op(dve, lambda h: h.tensor_scalar(
                            out=MODD.t[:, t_, 3 * n_ + 2, :], in0=MOD.t[:, base + 32:base + 48, t_],
                            scalar1=0.5, scalar2=None, op0=ALU.mult), MOD.b, MODD.b)
            k.barrier()
        tap("modd", MODD.t[:].rearrange("p a b c -> p (a b c)"), MODD.b)

        def mcol(t_, kind, c):
            return MODD.t[:, t_, kind, c:c + 1]

        g2 = {"next": NMOD1, "loaded": []}

        def gemv2_step():
            for (j, wv, wb) in g2["loaded"]:
                c0 = 2 * (j - NMOD1)
                for kc in range(NCK):
                    mm(stat_bank.t[:, c0:c0 + 2], wv[:, kc, :], SCV.t[:, kc, :], kc == 0, kc == NCK - 1,
                       [wb, SCV.b[0]], stat_bank.b)
            g2["loaded"] = []
            for _ in range(4):
                j = g2["next"]
                if j >= 144:
                    break
                g2["next"] += 1
                wv, wb = wload(colpanel(w_ada, j * 128), "p (k c) -> p k c", c=128)
                g2["loaded"].append((j, wv, wb))

        def gemv2_finish():
            while g2["loaded"] or g2["next"] < 144:
                gemv2_step()
            ps3 = stat_bank.t[:, 0:2 * (144 - NMOD1)].rearrange("p (j t) -> p j t", t=2)
            for t_ in range(2):
                bb = lambda a: PAR.t[:, P_BADA + NMOD1 + a:P_BADA + NMOD1 + a + 16]
                k.op(dve, lambda h: h.tensor_tensor(out=MODD.t[:, t_, 5, :], in0=ps3[:, 0:16, t_], in1=bb(0), op=ALU.add),
                     [stat_bank.b[0], pB], MODD.b)
                k.op(dve, lambda h: h.tensor_tensor(out=MODD.t[:, t_, 7, :], in0=ps3[:, 16:32, t_], in1=bb(16), op=ALU.add),
                     [stat_bank.b[0], pB], MODD.b)
                k.op(dve, lambda h: h.tensor_tensor(out=MODD.t[:, t_, 6, :], in0=ps3[:, 32:48, t_], in1=bb(32), op=ALU.add),
                     [stat_bank.b[0], pB], MODD.b)
                k.op(dve, lambda h: h.scalar_tensor_tensor(out=MODD.t[:, t_, 6, :], in0=MODD.t[:, t_, 6, :], scalar=1.0,
                                                           in1=PAR.t[:, P_N3:P_N3 + 16], op0=ALU.add, op1=ALU.mult), MODD.b + [pB], MODD.b)
                k.op(dve, lambda h: h.tensor_tensor(out=MODD.t[:, t_, 8, :], in0=ps3[:, 48:64, t_], in1=bb(48), op=ALU.add),
                     [stat_bank.b[0], pB], MODD.b)
                k.op(dve, lambda h: h.tensor_scalar(out=MODD.t[:, t_, 8, :], in0=MODD.t[:, t_, 8, :], scalar1=0.5, scalar2=None,
                                                    op0=ALU.mult), MODD.b, MODD.b)

        def load_x(blk, H):
            with ExitStack() as ph:
                XT = k.sb(ph, [128, NT, D], NT)
                for tt in range(NT):
                    k.spdma(lambda h: h.dma_start(out=XT.t[:, tt, :], in_=xs[blk, tt * 128:(tt + 1) * 128, :]),
                         [], [XT.b[tt]])
                n_ev = 0
                for tt in range(NT):
                    for q4 in range(4):
                        bk = bank()
                        for i in range(4):
                            c = 4 * q4 + i
                            k.op(pe, lambda h: h.transpose(bk.t[:, i * 128:(i + 1) * 128], XT.t[:, tt, c * 128:(c + 1) * 128], ident),
                                 [XT.b[tt], cB], bk.b, inc=(i == 3))
                        dst = H.t[:, 4 * q4:4 * q4 + 4, tt * 128:(tt + 1) * 128]
                        src = bk.t[:].rearrange("p (a b) -> p a b", b=128)
                        hb = [H.b[4 * q4 + i] for i in range(4)]
                        if n_ev % 2 == 0:
                            k.op(act, lambda h: h.activation(out=dst, in_=src, func=AF.Identity), bk.b, hb)
                        else:
                            k.op(dve, lambda h: h.tensor_copy(out=dst, in_=src), bk.b, hb)
                        n_ev += 1
                k.barrier()

        def rms_stats(src_fn, nchunk, reads_fn, RSTD, TMP):
            nb_ = len(TMP.b)
            for c in range(nchunk):
                tb = TMP.b[c % nb_]
                if c % 2 == 0:
                    k.op(act, lambda h: h.activation(out=R(TMP.t[:, c % nb_, :]), in_=src_fn(c), func=AF.Square),
                         reads_fn(c), [tb])
                else:
                    k.op(dve, lambda h: h.tensor_tensor(out=R(TMP.t[:, c % nb_, :]), in0=src_fn(c), in1=src_fn(c), op=ALU.mult),
                         reads_fn(c), [tb])
                mm(stat_bank.t[:], ones, TMP.t[:, c % nb_, :], c == 0, c == nchunk - 1, [cB, tb], stat_bank.b, inc=True)
            k.op(dve, lambda h: h.tensor_scalar(out=RSTD.t[:], in0=stat_bank.t[:], scalar1=1.0 / D, scalar2=EPS,
                                                op0=ALU.mult, op1=ALU.add), stat_bank.b, RSTD.b)
            k.op(act, lambda h: h.activation(out=RSTD.t[:], in_=RSTD.t[:], func=AF.Sqrt), RSTD.b, RSTD.b)
            k.op(dve, lambda h: h.reciprocal(out=RSTD.t[:], in_=RSTD.t[:]), RSTD.b, RSTD.b)

        def norm_mod(H, U, t_, n_, ph):
            with ExitStack() as p2:
                RSTD = k.sb(p2, [128, T])
                TMP = k.sb(p2, [128, 4, T], 4)
                SQT = k.sb(p2, [128, 4, T], 4)
                rms_stats(lambda c: H.t[:, c, :], NCK, lambda c: [H.b[c]], RSTD, SQT)
                for c in range(NCK):
                    tb = TMP.b[c % 4]
                    tv = TMP.t[:, c % 4, :]
                    if c % 2 == 0:
                        k.op(dve, lambda h: h.scalar_tensor_tensor(
                            out=tv, in0=H.t[:, c, :], scalar=mcol(t_, 3 * n_, c), in1=RSTD.t[:],
                            op0=ALU.mult, op1=ALU.mult), [H.b[c], MODD.b[0], RSTD.b[0]], [tb])
                        k.op(act, lambda h: h.activation(out=R(U.t[:, c, :]), in_=tv, func=AF.Identity,
                                                         bias=mcol(t_, 3 * n_ + 1, c), scale=1.0),
                             [tb, MODD.b[0]], [U.b[c]])
                    else:
                        k.op(pool, lambda h: h.tensor_tensor(out=tv, in0=H.t[:, c, :], in1=RSTD.t[:], op=ALU.mult),
                             [H.b[c], RSTD.b[0]], [tb])
                        k.op(act, lambda h: h.activation(out=R(U.t[:, c, :]), in_=tv, func=AF.Identity,
                                                         bias=mcol(t_, 3 * n_ + 1, c), scale=mcol(t_, 3 * n_, c)),
                             [tb, MODD.b[0]], [U.b[c]])
                k.barrier()

        def ffn(H, U, t_, n_, wg, wu, wd):
            G = 2
            with ExitStack() as ph:
                HID = k.sb(ph, [128, 2 * G, T], 2 * G)
                SG = k.sb(ph, [128, 2, T], 2)
                xs_ = extra_slots(ph, 4)
                for fg in range(DFF // 128 // G):
                    wds = []
                    for i in range(G):
                        f = fg * G + i
                        hs = (fg % 2) * G + i
                        gv, gb = wload(colpanel(wg, f * 128), "p (k c) -> p k c", c=128)
                        uv, ub = wload(colpanel(wu, f * 128), "p (k c) -> p k c", c=128)
                        wds.append(wload(wd[f * 128:(f + 1) * 128, :]))
                        gbk, ubk = bank(), bank()
                        for kc in range(NCK):
                            mm(gbk.t[:], gv[:, kc, :], U.t[:, kc, :], kc == 0, kc == NCK - 1, [gb, U.b[kc]], gbk.b)
                        for kc in range(NCK):
                            mm(ubk.t[:], uv[:, kc, :], U.t[:, kc, :], kc == 0, kc == NCK - 1, [ub, U.b[kc]], ubk.b)
                        k.op(act, lambda h: h.activation(out=SG.t[:, i, :], in_=gbk.t[:], func=AF.Silu), gbk.b, [SG.b[i]])
                        k.op(dve, lambda h: h.tensor_tensor(out=R(HID.t[:, hs, :]), in0=SG.t[:, i, :], in1=ubk.t[:], op=ALU.mult),
                             [SG.b[i], ubk.b[0]], [HID.b[hs]])
                    for d_ in range(NCK):
                        ob = bank()
                        for i in range(G):
                            hs = (fg % 2) * G + i
                            mm(ob.t[:], wds[i][0][:, d_ * 128:(d_ + 1) * 128], HID.t[:, hs, :], i == 0, i == G - 1,
                               [wds[i][1], HID.b[hs]], ob.b)
                        k.op(dve, lambda h: h.scalar_tensor_tensor(
                            out=H.t[:, d_, :], in0=ob.t[:], scalar=mcol(t_, 3 * n_ + 2, d_), in1=H.t[:, d_, :],
                            op0=ALU.mult, op1=ALU.add), [ob.b[0], MODD.b[0], H.b[d_]], [H.b[d_]])
                xs_.close()
                k.barrier()

        def proj_spill(blk, U, chunks):
            with ExitStack() as ph:
                STG = k.sb(ph, [128, 3, T], 3)
                for n, (col, slot_idx) in enumerate(chunks):
                    wv, wb = wload(colpanel(w_in, col), "p (k c) -> p k c", c=128)
                    bk = bank()
                    for kc in range(NCK):
                        mm(bk.t[:], wv[:, kc, :], U.t[:, kc, :], kc == 0, kc == NCK - 1, [wb, U.b[kc]], bk.b)
                    s = n % 3
                    k.op(act, lambda h: h.activation(out=STG.t[:, s, :], in_=bk.t[:], func=AF.Identity), bk.b, [STG.b[s]])
                    if blk < 4 and slot_idx < 24:
                        k.op(dve, lambda h: h.tensor_copy(out=EDGE.t[:, blk, slot_idx, :], in_=STG.t[:, s, 0:T:T - 1]),
                             [STG.b[s]], [EDGE.b[blk]])
                    k.spdma(lambda h: h.dma_start(out=scr[blk, slot_idx], in_=STG.t[:, s, :]), [STG.b[s]], [scrB[blk][slot_idx]])
                k.barrier()

        scrB = [[Buf() for _ in range(40)] for _ in range(5)]

        def proj_dt(blk, U):
            with ExitStack() as ph:
                TMPD = k.sb(ph, [128, NT, 64])
                wdv, wdb = wload(w_dt, "p (k c) -> p k c", c=64)
                bk = bank()
                for tt in range(NT):
                    for kc in range(NCK):
                        mm(bk.t[:, tt * 64:(tt + 1) * 64], U.t[:, kc, tt * 128:(tt + 1) * 128], wdv[:, kc, :],
                           kc == 0, kc == NCK - 1, [U.b[kc], wdb], bk.b)
                k.op(dve, lambda h: h.tensor_tensor(
                    out=TMPD.t[:], in0=bk.t[:, 0:NT * 64].rearrange("p (t c) -> p t c", c=64),
                    in1=PAR.t[:, P_DTB:P_DTB + 64].unsqueeze(1).to_broadcast([128, NT, 64]), op=ALU.add),
                    [bk.b[0], pB], TMPD.b)
                k.op(act, lambda h: h.activation(out=TMPD.t[:], in_=TMPD.t[:], func=AF.Exp), TMPD.b, TMPD.b)
                k.op(act, lambda h: h.activation(out=DT.t[:, blk], in_=TMPD.t[:], func=AF.Ln, bias=1.0, scale=1.0),
                     TMPD.b, [DT.b[blk]])
                k.op(dve, lambda h: h.tensor_tensor(
                    out=AA.t[:, blk], in0=DT.t[:, blk], in1=ANEG.t[:].unsqueeze(1).to_broadcast([128, NT, 64]), op=ALU.mult),
                    [DT.b[blk], ANEG.b[0]], [AA.b[blk]])
                k.barrier()


        def front(blk, H, t_, with_z):
            with ExitStack() as ph:
                U = k.sb(ph, [128, NCK, T], NCK)
                load_x(blk, H)
                if SUB >= 2:
                    norm_mod(H, U, t_, 0, ph)
                if SUB >= 3:
                    ffn(H, U, t_, 0, f1g, f1u, f1d)
                if SUB >= 4:
                    norm_mod(H, U, t_, 1, ph)
                chunks = [(C_XBC + j * 128, j) for j in range(24)]
                if with_z:
                    chunks += [(C_Z + j * 128, 24 + j) for j in range(16)]
                if SUB >= 5:
                    proj_spill(blk, U, chunks)
                if SUB >= 6:
                    proj_dt(blk, U)
                if SUB < 6:
                    tap("u", U.t[:].rearrange("p a b -> p (a b)"), U.b)
                k.barrier()

        def make_dec(blk, DEC):
            for tt in range(NT):
                bk = bank()
                for i, tri in enumerate((Tle, Tgt, Tge, Tlt, ones)):
                    mm(bk.t[:, i * 64:(i + 1) * 64], tri, AA.t[:, blk, tt, :], True, True, [cB, AA.b[blk]], bk.b,
                       r=False, inc=(i == 4))
                k.op(act, lambda h: h.activation(out=DEC.t[:, tt], in_=bk.t[:, 0:320].rearrange("p (a c) -> p a c", c=64),
                                                 func=AF.Exp), bk.b, DEC.b)

        def prep(blk, g, W, need_c, seqs, halo):
            XG, BC, CT_, XTOK, BTOK, PRE = W["XG"], W["BC"], W["CTMP"], W["XTOK"], W["BTOK"], W["PRE"]
            srcs = [(4 * g + i, XG.t[:, i, :], XG.b[i], True) for i in range(4)] + [(16 + g, BC.t[:, 0, :], BC.b[0], True)]
            if need_c:
                srcs.append((20 + g, BC.t[:, 1, :], BC.b[1], True))
            L = T // seqs
            nct = CT_.t.shape[1]
            for i2 in range(2):
                k.op(dve, lambda h: h.memset(PRE.t[:, i2, 0:seqs * (L + 2)].rearrange("p (s l) -> p s l", l=L + 2)[:, :, 0:L + 2:L + 1], 0.0),
                     [], [PRE.b[i2]])
            for n_src, (j, fin, fb, as_r) in enumerate(srcs):
                P3 = PRE.t[:, n_src % 2, 0:seqs * (L + 2)].rearrange("p (s l) -> p s l", l=L + 2)
                db = PRE.b[n_src % 2]
                k.spdma(lambda h: h.dma_start(out=P3[:, :, 1:L + 1], in_=scr[blk, j].rearrange("p (s l) -> p s l", l=L)),
                     [scrB[blk][j]], [db])
                w0 = PAR.t[:, P_SCW + 3 * j:P_SCW + 3 * j + 1]
                w1 = PAR.t[:, P_SCW + 3 * j + 1:P_SCW + 3 * j + 2]
                w2 = PAR.t[:, P_SCW + 3 * j + 2:P_SCW + 3 * j + 3]
                cbias = PAR.t[:, P_SCB + j:P_SCB + j + 1]
                acc = CT_.t[:, n_src % nct, :]
                ab = [CT_.b[n_src % nct]]
                a3 = acc.rearrange("p (s l) -> p s l", l=L)
                if halo is not None:
                    li, ri, ml, mr = halo
                    k.op(dve, lambda h: h.tensor_scalar(out=P3[:, 0, 0:1], in0=EDGE.t[:, li, j, 1:2], scalar1=ml, scalar2=None,
                                                        op0=ALU.mult), [EDGE.b[li], pB], [db])
                    k.op(dve, lambda h: h.tensor_scalar(out=P3[:, 0, L + 1:L + 2], in0=EDGE.t[:, ri, j, 0:1], scalar1=mr, scalar2=None,
                                                        op0=ALU.mult), [EDGE.b[ri], pB], [db])
                k.op(dve, lambda h: h.tensor_scalar(out=a3, in0=P3[:, :, 1:L + 1], scalar1=w1, scalar2=None, op0=ALU.mult), [db, pB], ab)
                k.op(dve, lambda h: h.scalar_tensor_tensor(out=a3, in0=P3[:, :, 0:L], scalar=w0, in1=a3, op0=ALU.mult, op1=ALU.add),
                     [db, pB] + ab, ab)
                k.op(dve, lambda h: h.scalar_tensor_tensor(out=a3, in0=P3[:, :, 2:L + 2], scalar=w2, in1=a3, op0=ALU.mult, op1=ALU.add),
                     [db, pB] + ab, ab)
                k.op(act, lambda h: h.activation(out=(R(fin) if as_r else fin), in_=acc, func=AF.Silu, bias=cbias, scale=1.0),
                     ab + [pB], [fb])
            for tt in range(NT):
                bk = bank()
                for i in range(4):
                    k.op(pe, lambda h: h.transpose(bk.t[:, i * 128:(i + 1) * 128], XG.t[:, i, tt * 128:(tt + 1) * 128], ident),
                         [XG.b[i], cB], bk.b, inc=(i == 3))
                k.op(act, lambda h: h.activation(out=XTOK.t[:, tt, :], in_=bk.t[:], func=AF.Identity), bk.b, [XTOK.b[tt]])
            bk = bank()
            for tt in range(NT):
                k.op(pe, lambda h: h.transpose(bk.t[:, tt * 128:(tt + 1) * 128], BC.t[:, 0, tt * 128:(tt + 1) * 128], ident),
                     [BC.b[0], cB], bk.b, inc=(tt == NT - 1))
            k.op(act, lambda h: h.activation(out=R(BTOK.t[:].rearrange("p a b -> p (a b)")), in_=bk.t[:], func=AF.Identity),
                 bk.b, BTOK.b)

        def ssd_A(blk, g, tt, dr, Wg, Wc, DEC):
            XTOK, BTOK, BC = Wg["XTOK"], Wg["BTOK"], Wg["BC"]
            XD, XDD, RM, LM, CBM = Wc["XD"], Wc["XDD"], Wc["RM"], Wc["LM"], Wc["CBM"]
            hs = slice(dr * 32 + g * 8, dr * 32 + g * 8 + 8)
            x3 = XTOK.t[:, tt, :].rearrange("p (h d) -> p h d", d=64)
            bc8 = lambda ap: ap.unsqueeze(2).to_broadcast([128, 8, 64])
            k.op(dve, lambda h: h.tensor_tensor(out=R(XD.t[:].rearrange("p (h d) -> p h d", d=64)), in0=x3,
                                                in1=bc8(DT.t[:, blk, tt, hs]), op=ALU.mult), [XTOK.b[tt], DT.b[blk]], XD.b)
            k.op(dve, lambda h: h.tensor_tensor(out=R(XDD.t[:].rearrange("p (h d) -> p h d", d=64)),
                                                in0=XD.t[:].rearrange("p (h d) -> p h d", d=64),
                                                in1=bc8(DEC.t[:, tt, 1 + 2 * dr, hs]), op=ALU.mult), XD.b + DEC.b, XDD.b)
            tri_l, tri_r = (Tgt, Tle) if dr == 0 else (Tlt, Tge)
            k.op(dve, lambda h: h.tensor_tensor(
                out=R(RM.t[:]), in0=AA.t[:, blk, tt, hs].unsqueeze(2).to_broadcast([128, 8, 128]),
                in1=tri_r.unsqueeze(1).to_broadcast([128, 8, 128]), op=ALU.mult), [AA.b[blk], cB], RM.b)
            for half in range(2):
                zb = bank()
                mm(zb.t[:], tri_l, RM.t[:, 4 * half:4 * half + 4, :].rearrange("p a b -> p (a b)"), True, True,
                   [cB, RM.b[0]], zb.b)
                k.op(act, lambda h: h.activation(out=R(LM.t[:, 4 * half:4 * half + 4, :].rearrange("p a b -> p (a b)")),
                                                 in_=zb.t[:], func=AF.Exp), zb.b, LM.b)
            cb_ = bank()
            mm(cb_.t[:, 0:128], BC.t[:, 0, tt * 128:(tt + 1) * 128], BC.t[:, 1, tt * 128:(tt + 1) * 128], True, True,
               [BC.b[0], BC.b[1]], cb_.b)
            tri_m = Tle if dr == 0 else Tge
            k.op(dve, lambda h: h.tensor_tensor(out=CBM.t[:], in0=cb_.t[:, 0:128], in1=tri_m, op=ALU.mult), [cb_.b[0], cB], CBM.b)
            return None

        def ssd_B(blk, g, tt, dr, Wg, Wc, DEC, S, s_zero, YACC, carry):
            BC, BTOK = Wg["BC"], Wg["BTOK"]
            XD, XDD, LM, CBM, FT = Wc["XD"], Wc["XDD"], Wc["LM"], Wc["CBM"], Wc["FT"]
            hs = slice(dr * 32 + g * 8, dr * 32 + g * 8 + 8)
            sap, sbuf = S
            bc8 = lambda ap: ap.unsqueeze(2).to_broadcast([128, 8, 64])
            k.op(dve, lambda h: h.tensor_tensor(out=R(LM.t[:]), in0=LM.t[:], in1=CBM.t[:].unsqueeze(1).to_broadcast([128, 8, 128]),
                                                op=ALU.mult), LM.b + CBM.b, LM.b)
            yb = bank()
            for hh in range(8):
                mm(yb.t[:, hh * 64:(hh + 1) * 64], LM.t[:, hh, :], XD.t[:, hh * 64:(hh + 1) * 64], True, True,
                   [LM.b[0], XD.b[0]], yb.b, inc=(hh == 7))
            k.op(dve, lambda h: h.tensor_tensor(out=YACC.t[:, tt, :], in0=YACC.t[:, tt, :], in1=yb.t[:], op=ALU.add),
                 [YACC.b[tt], yb.b[0]], [YACC.b[tt]])
            if not s_zero:
                ob = bank()
                mm(ob.t[:], BC.t[:, 1, tt * 128:(tt + 1) * 128], sap, True, True, [BC.b[1], sbuf], ob.b)
                k.op(dve, lambda h: h.tensor_tensor(out=FT.t[:].rearrange("p (h d) -> p h d", d=64),
                                                    in0=ob.t[:].rearrange("p (h d) -> p h d", d=64),
                                                    in1=bc8(DEC.t[:, tt, 2 * dr, hs]), op=ALU.mult),
                     [ob.b[0]] + DEC.b, FT.b)
                k.op(dve, lambda h: h.tensor_tensor(out=YACC.t[:, tt, :], in0=YACC.t[:, tt, :], in1=FT.t[:], op=ALU.add),
                     [YACC.b[tt]] + FT.b, [YACC.b[tt]])
            sb_ = bank()
            mm(sb_.t[:], BTOK.t[:, tt, :], XDD.t[:], True, True, [BTOK.b[0], XDD.b[0]], sb_.b)
            if s_zero:
                k.op(act, lambda h: h.activation(out=R(sap), in_=sb_.t[:], func=AF.Identity), sb_.b, [sbuf])
            else:
                k.op(dve, lambda h: h.tensor_tensor(out=FT.t[:].rearrange("p (h d) -> p h d", d=64),
                                                    in0=sap.rearrange("p (h d) -> p h d", d=64),
                                                    in1=bc8(DEC.t[:, tt, 4, hs]), op=ALU.mult), [sbuf] + DEC.b, FT.b)
                k.op(dve, lambda h: h.tensor_tensor(out=R(sap), in0=FT.t[:], in1=sb_.t[:], op=ALU.add), FT.b + [sb_.b[0]], [sbuf])

        def ssd_run(blk, g, Wg, chk, DEC, YACC, items):
            n = len(items)
            carry = [None] * n
            skew = 1 if len(chk) > 1 else 0
            for i in range(n + skew):
                if i < n:
                    tt, dr, S, s_zero, post = items[i]
                    carry[i] = ssd_A(blk, g, tt, dr, Wg, chk[i % len(chk)], DEC)
                j = i - skew
                if j >= 0:
                    tt, dr, S, s_zero, post = items[j]
                    ssd_B(blk, g, tt, dr, Wg, chk[j % len(chk)], DEC, S, s_zero, YACC, carry[j])
                    if post is not None:
                        post()

        def alloc_work(ph, full, ng, ncb):
            grp = [{"XG": k.sb(ph, [128, 4, T], 4), "BC": k.sb(ph, [128, 2, T], 2), "CTMP": k.sb(ph, [128, 2, T], 2),
                    "PRE": k.sb(ph, [128, 2, T + 4], 2), "XTOK": k.sb(ph, [128, NT, T], NT), "BTOK": k.sb(ph, [128, NT, 128])}
                   for _ in range(ng)]
            chk = []
            for i in range(ncb):
                need = (3 * T + (2 * 8 * 128 + 128 if full else 0)) * 4
                if i > 0 and nc.sbuf_bytes_remaining < need:
                    break
                c = {"XD": k.sb(ph, [128, T]), "XDD": k.sb(ph, [128, T]), "FT": k.sb(ph, [128, T])}
                if full:
                    c.update({"RM": k.sb(ph, [128, 8, 128]), "LM": k.sb(ph, [128, 8, 128]), "CBM": k.sb(ph, [128, 128])})
                chk.append(c)
            return grp, chk

        def state_pass():
            with ExitStack() as ph:
                grp, _ = alloc_work(ph, False, 1, 0)
                g2 = dict(grp[0])
                g2["XTOK"], g2["BTOK"] = k.sb(ph, [128, NT, T], NT), k.sb(ph, [128, NT, 128])
                grp.append(g2)
                DEC = k.sb(ph, [128, NT, 5, 64])
                SPF = k.sb(ph, [128, NT, 64])
                MS = k.sb(ph, [128, NT, 64])
                DSEG = k.sb(ph, [128, 4, 64], 4)
                XDDS = k.sb(ph, [128, 4, T], 4)
                FT = k.sb(ph, [128, 2, T], 2)
                SSB = k.sb(ph, [128, 3, 4, T], 12)
                H0T = k.sb(ph, [128, 2, 4, T], 8)
                STG = k.sb(ph, [128, 4, 128])
                for dr in range(2):
                    for g in range(4):
                        k.spdma(lambda h: h.dma_start(out=STG.t[:], in_=h0d[dr, g * 512:(g + 1) * 512, :].rearrange("(a p) n -> p a n", p=128)),
                             [], STG.b)
                        bk = bank()
                        for i in range(4):
                            k.op(pe, lambda h: h.transpose(bk.t[:, i * 128:(i + 1) * 128], STG.t[:, i, :], ident),
                                 STG.b + [cB], bk.b, inc=(i == 3))
                        k.op(act, lambda h: h.activation(out=H0T.t[:, dr, g, :], in_=bk.t[:], func=AF.Identity), bk.b, [H0T.b[dr * 4 + g]])
                mk = lambda i: PAR.t[:, P_MSK + i:P_MSK + i + 1]
                bc8 = lambda ap: ap.unsqueeze(2).to_broadcast([128, 8, 64])
                cnt = {"x": 0, "f": 0, "g": 0}

                def step(HH, dr, g, idx, first, dseg_ap, dseg_b, add_ap, add_b):
                    wrap0, keep0 = (8, 12) if dr == 0 else (16, 20)
                    ft, fb = FT.t[:, cnt["f"] % 2, :], FT.b[cnt["f"] % 2]
                    cnt["f"] += 1
                    if first:
                        k.op(dve, lambda h: h.tensor_scalar(out=R(HH.t[:, g, :]), in0=H0T.t[:, dr, g, :], scalar1=mk(wrap0 + idx),
                                                            scalar2=None, op0=ALU.mult), [H0T.b[dr * 4 + g], pB], [HH.b[g]])
                    else:
                        k.op(dve, lambda h: h.tensor_scalar(out=ft, in0=HH.t[:, g, :], scalar1=mk(keep0 + idx),
                                                            scalar2=None, op0=ALU.mult), [HH.b[g], pB], [fb])
                        k.op(dve, lambda h: h.scalar_tensor_tensor(out=R(HH.t[:, g, :]), in0=H0T.t[:, dr, g, :], scalar=mk(wrap0 + idx),
                                                                   in1=ft, op0=ALU.mult, op1=ALU.add),
                             [H0T.b[dr * 4 + g], pB, fb], [HH.b[g]])
                    if add_ap is None:
                        return
                    k.op(dve, lambda h: h.tensor_tensor(out=ft.rearrange("p (h d) -> p h d", d=64),
                                                        in0=HH.t[:, g, :].rearrange("p (h d) -> p h d", d=64),
                                                        in1=bc8(dseg_ap), op=ALU.mult), [HH.b[g]] + dseg_b, [fb])
                    k.op(dve, lambda h: h.tensor_tensor(out=R(HH.t[:, g, :]), in0=ft, in1=add_ap, op=ALU.add), [fb] + add_b, [HH.b[g]])

                for si, seg in enumerate([1, 2, 3]):
                    make_dec(seg, DEC)
                    gemv2_step()
                    k.op(dve, lambda h: h.memset(SPF.t[:], 1.0), [], SPF.b)
                    for tt in (2, 1, 0):
                        k.op(dve, lambda h: h.tensor_tensor(out=SPF.t[:, tt, 0:32], in0=SPF.t[:, tt + 1, 0:32], in1=DEC.t[:, tt + 1, 4, 0:32], op=ALU.mult),
                             SPF.b + DEC.b, SPF.b)
                    for tt in (1, 2, 3):
                        k.op(dve, lambda h: h.tensor_tensor(out=SPF.t[:, tt, 32:64], in0=SPF.t[:, tt - 1, 32:64], in1=DEC.t[:, tt - 1, 4, 32:64], op=ALU.mult),
                             SPF.b + DEC.b, SPF.b)
                    k.op(dve, lambda h: h.tensor_tensor(out=DSEG.t[:, seg, 0:32], in0=SPF.t[:, 0, 0:32], in1=DEC.t[:, 0, 4, 0:32], op=ALU.mult),
                         SPF.b + DEC.b, [DSEG.b[seg]])
                    k.op(dve, lambda h: h.tensor_tensor(out=DSEG.t[:, seg, 32:64], in0=SPF.t[:, 3, 32:64], in1=DEC.t[:, 3, 4, 32:64], op=ALU.mult),
                         SPF.b + DEC.b, [DSEG.b[seg]])
                    k.op(dve, lambda h: h.tensor_tensor(out=MS.t[:], in0=DT.t[:, seg], in1=SPF.t[:], op=ALU.mult), [DT.b[seg]] + SPF.b, MS.b)
                    k.op(dve, lambda h: h.tensor_tensor(out=MS.t[:, :, 0:32], in0=MS.t[:, :, 0:32], in1=DEC.t[:, :, 1, 0:32], op=ALU.mult), MS.b + DEC.b, MS.b)
                    k.op(dve, lambda h: h.tensor_tensor(out=MS.t[:, :, 32:64], in0=MS.t[:, :, 32:64], in1=DEC.t[:, :, 3, 32:64], op=ALU.mult), MS.b + DEC.b, MS.b)
                    for g in range(4):
                        Wg = grp[cnt["g"] % len(grp)]
                        cnt["g"] += 1
                        li, ri = (seg - 1) % 4, (seg + 1) % 4
                        prep(seg, g, Wg, False, 1, (li, ri, mk(seg), mk(4 + seg)))
                        for dr in range(2):
                            sbk = bank()
                            hs = slice(dr * 32 + g * 8, dr * 32 + g * 8 + 8)
                            for tt in range(NT):
                                xi = cnt["x"] % 4
                                cnt["x"] += 1
                                k.op(dve, lambda h: h.tensor_tensor(out=R(XDDS.t[:, xi, :].rearrange("p (h d) -> p h d", d=64)),
                                                                    in0=Wg["XTOK"].t[:, tt, :].rearrange("p (h d) -> p h d", d=64),
                                                                    in1=bc8(MS.t[:, tt, hs]), op=ALU.mult),
                                     [Wg["XTOK"].b[tt]] + MS.b, [XDDS.b[xi]])
                                mm(sbk.t[:], Wg["BTOK"].t[:, tt, :], XDDS.t[:, xi, :], tt == 0, tt == NT - 1,
                                   [Wg["BTOK"].b[0], XDDS.b[xi]], sbk.b, inc=True)
                            if dr == 0:
                                step(HF, 0, g, si, si == 0, DSEG.t[:, seg, hs], [DSEG.b[seg]], sbk.t[:], sbk.b)
                            else:
                                k.op(act, lambda h: h.activation(out=SSB.t[:, seg - 1, g, :], in_=sbk.t[:], func=AF.Identity),
                                     sbk.b, [SSB.b[(seg - 1) * 4 + g]])
                        gemv2_step()
                for si, seg in enumerate([3, 2, 1]):
                    for g in range(4):
                        hs = slice(32 + g * 8, 32 + g * 8 + 8)
                        step(HB, 1, g, si, si == 0, DSEG.t[:, seg, hs], [DSEG.b[seg]], SSB.t[:, seg - 1, g, :], [SSB.b[(seg - 1) * 4 + g]])
                for g in range(4):
                    step(HF, 0, g, 3, False, None, None, None, None)
                    step(HB, 1, g, 3, False, None, None, None, None)
                gemv2_finish()
                k.barrier()

        def back(blk, H, t_, is_sample, y_out):
            with ExitStack() as ph:
                ACC = k.sb(ph, [128, NCK, T], NCK)
                RSTD = k.sb(ph, [128, T])
                with ExitStack() as p2:
                    DEC = k.sb(p2, [128, NT, 5, 64])
                    YACC = k.sb(p2, [128, NT, T], NT)
                    SQ = k.sb(p2, [128, 1, T], 1)
                    STO = k.sb(p2, [128, 4, 128])
                    grp, chk = alloc_work(p2, True, 1, 2)
                    W = grp[0]
                    YN = W["XG"]
                    cc_ = {"c": 0}
                    if VERBOSE:
                        print("back: chunk buffer sets", len(chk), "sbuf left", nc.sbuf_bytes_remaining)

                    def nxt():
                        cc_["c"] += 1
                        return chk[cc_["c"] % len(chk)]
                    make_dec(blk, DEC)
                    nsq = 0
                    for g in range(4):
                        wps = [wload(wso[(4 * g + i) * 128:(4 * g + i + 1) * 128, :]) for i in range(4)]
                        halo = (3, 1, PAR.t[:, P_MSK:P_MSK + 1], PAR.t[:, P_MSK + 4:P_MSK + 5]) if is_sample else None
                        prep(blk, g, W, True, 1 if is_sample else 2, halo)
                        for tt in range(NT):
                            k.op(dve, lambda h: h.tensor_tensor(
                                out=YACC.t[:, tt, :].rearrange("p (h d) -> p h d", d=64),
                                in0=W["XTOK"].t[:, tt, :].rearrange("p (h d) -> p h d", d=64),
                                in1=PAR.t[:, P_DSK + 8 * g:P_DSK + 8 * g + 8].unsqueeze(2).to_broadcast([128, 8, 64]), op=ALU.mult),
                                [W["XTOK"].b[tt], pB], [YACC.b[tt]])
                        items = []
                        for dr in range(2):
                            if is_sample:
                                HH = HF if dr == 0 else HB
                                for tt in (range(NT) if dr == 0 else range(NT - 1, -1, -1)):
                                    items.append((tt, dr, (HH.t[:, g, :], HH.b[g]), False, None))
                            else:
                                for sq in range(2):
                                    tts = [2 * sq, 2 * sq + 1] if dr == 0 else [2 * sq + 1, 2 * sq]

                                    def emit_state(dr=dr, sq=sq):
                                        bk = bank()
                                        for i in range(4):
                                            k.op(pe, lambda h: h.transpose(bk.t[:, i * 128:(i + 1) * 128], HF.t[:, dr, i * 128:(i + 1) * 128], ident),
                                                 [HF.b[dr], cB], bk.b, inc=(i == 3))
                                        k.op(act, lambda h: h.activation(out=STO.t[:].rearrange("p a b -> p (a b)"), in_=bk.t[:], func=AF.Identity),
                                             bk.b, STO.b)
                                        dst = (st_f if dr == 0 else st_b)[sq, g * 512:(g + 1) * 512, :].rearrange("(a p) n -> p a n", p=128)
                                        k.spdma(lambda h: h.dma_start(out=dst, in_=STO.t[:]), STO.b, [Buf()])
                                    for n_, tt in enumerate(tts):
                                        items.append((tt, dr, (HF.t[:, dr, :], HF.b[dr]), n_ == 0, emit_state if n_ == 1 else None))
                        ssd_run(blk, g, W, chk, DEC, YACC, items)
                        PRE = W["PRE"]
                        for i in range(4):
                            zt, zb_ = PRE.t[:, i % 2, 0:T], PRE.b[i % 2]
                            k.spdma(lambda h: h.dma_start(out=zt, in_=scr[blk, 24 + 4 * g + i]), [scrB[blk][24 + 4 * g + i]], [zb_])
                            bk = bank()
                            for tt in range(NT):
                                k.op(pe, lambda h: h.transpose(bk.t[:, tt * 128:(tt + 1) * 128], YACC.t[:, tt, i * 128:(i + 1) * 128], ident),
                                     [YACC.b[tt], cB], bk.b, inc=(tt == NT - 1))
                            k.op(act, lambda h: h.activation(out=zt, in_=zt, func=AF.Silu), [zb_], [zb_])
                            k.op(dve, lambda h: h.tensor_tensor(out=zt, in0=zt, in1=bk.t[:], op=ALU.mult), [zb_, bk.b[0]], [zb_])
                            sqb = SQ.b[0]
                            k.op(act, lambda h: h.activation(out=R(SQ.t[:, 0, :]), in_=zt, func=AF.Square), [zb_], [sqb])
                            mm(stat_bank.t[:], ones, SQ.t[:, 0, :], nsq == 0, nsq == 15, [cB, sqb], stat_bank.b, inc=True)
                            nsq += 1
                            c = 4 * g + i
                            k.op(dve, lambda h: h.tensor_scalar(out=R(YN.t[:, i, :]), in0=zt, scalar1=PAR.t[:, P_SN + c:P_SN + c + 1],
                                                                scalar2=None, op0=ALU.mult), [zb_, pB], [YN.b[i]])
                        for d_ in range(NCK):
                            ob = bank()
                            for i in range(4):
                                mm(ob.t[:], wps[i][0][:, d_ * 128:(d_ + 1) * 128], YN.t[:, i, :], i == 0, i == 3, [wps[i][1], YN.b[i]], ob.b)
                            if g == 0:
                                k.op(act, lambda h: h.activation(out=R(ACC.t[:, d_, :]), in_=ob.t[:], func=AF.Identity), ob.b, [ACC.b[d_]])
                            else:
                                k.op(dve, lambda h: h.tensor_tensor(out=R(ACC.t[:, d_, :]), in0=ACC.t[:, d_, :], in1=ob.t[:], op=ALU.add),
                                     [ACC.b[d_], ob.b[0]], [ACC.b[d_]])
                    k.op(dve, lambda h: h.tensor_scalar(out=RSTD.t[:], in0=stat_bank.t[:], scalar1=1.0 / D, scalar2=EPS,
                                                        op0=ALU.mult, op1=ALU.add), stat_bank.b, RSTD.b)
                    k.op(act, lambda h: h.activation(out=RSTD.t[:], in_=RSTD.t[:], func=AF.Sqrt), RSTD.b, RSTD.b)
                    k.op(dve, lambda h: h.reciprocal(out=RSTD.t[:], in_=RSTD.t[:]), RSTD.b, RSTD.b)
                    k.barrier()
                with ExitStack() as p2:
                    U = k.sb(p2, [128, NCK, T], NCK)
                    norm_mod(H, U, t_, 1, p2)
                    with ExitStack() as p3:
                        TM = k.sb(p3, [128, 4, T], 4)
                        NH = 1 if nc.sbuf_bytes_remaining >= 16 * T * 4 else 2
                        CPH = NCK // NH
                        if VERBOSE:
                            print('conv passes', NH)
                        PP = k.sb(p3, [128, CPH, T], CPH)
                        for d_ in range(NCK):
                            wv, wb = wload(colpanel(w_in, C_GS + d_ * 128), "p (k c) -> p k c", c=128)
                            bk = bank()
                            for kc in range(NCK):
                                mm(bk.t[:], wv[:, kc, :], U.t[:, kc, :], kc == 0, kc == NCK - 1, [wb, U.b[kc]], bk.b)
                            s = d_ % 2
                            k.op(act, lambda h: h.activation(out=TM.t[:, s, :], in_=bk.t[:], func=AF.Sigmoid), bk.b, [TM.b[s]])
                            k.op(dve, lambda h: h.tensor_tensor(out=TM.t[:, s, :], in0=TM.t[:, s, :], in1=RSTD.t[:], op=ALU.mult),
                                 [TM.b[s], RSTD.b[0]], [TM.b[s]])
                            k.op(dve, lambda h: h.tensor_tensor(out=R(ACC.t[:, d_, :]), in0=ACC.t[:, d_, :], in1=TM.t[:, s, :], op=ALU.mult),
                                 [ACC.b[d_], TM.b[s]], [ACC.b[d_]])
                        for half in range(NH):
                            Lr = 64 if is_sample else 256
                            for c in range(CPH * half, CPH * half + CPH):
                                bks = []
                                for col in (C_CC, C_CX, C_CB):
                                    wv, wb = wload(colpanel(w_in, col + c * 128), "p (k c) -> p k c", c=128)
                                    bk = bank()
                                    for kc in range(NCK):
                                        mm(bk.t[:], wv[:, kc, :], U.t[:, kc, :], kc == 0, kc == NCK - 1, [wb, U.b[kc]], bk.b)
                                    bks.append(bk)
                                pr, vv = TM.t[:, 2, :], TM.t[:, 3, :]
                                k.op(act, lambda h: h.activation(out=pr, in_=bks[0].t[:], func=AF.Identity), bks[0].b, [TM.b[2]])
                                k.op(dve, lambda h: h.tensor_tensor(out=pr, in0=pr, in1=bks[1].t[:], op=ALU.mult), [TM.b[2], bks[1].b[0]], [TM.b[2]])
                                w0, w1, w2 = (PAR.t[:, P_CW + 3 * c + i:P_CW + 3 * c + i + 1] for i in range(3))
                                p3 = pr.rearrange("p (s l) -> p s l", l=Lr)
                                v3 = vv.rearrange("p (s l) -> p s l", l=Lr)
                                k.op(dve, lambda h: h.tensor_scalar(out=vv, in0=pr, scalar1=w1, scalar2=None, op0=ALU.mult), [TM.b[2], pB], [TM.b[3]])
                                k.op(dve, lambda h: h.scalar_tensor_tensor(out=v3[:, :, 1:Lr], in0=p3[:, :, 0:Lr - 1], scalar=w0, in1=v3[:, :, 1:Lr],
                                                                           op0=ALU.mult, op1=ALU.add), [TM.b[2], TM.b[3], pB], [TM.b[3]])
                                k.op(dve, lambda h: h.scalar_tensor_tensor(out=v3[:, :, 0:Lr - 1], in0=p3[:, :, 1:Lr], scalar=w2, in1=v3[:, :, 0:Lr - 1],
                                                                           op0=ALU.mult, op1=ALU.add), [TM.b[2], TM.b[3], pB], [TM.b[3]])
                                k.op(dve, lambda h: h.tensor_tensor(out=R(PP.t[:, c % CPH, :]), in0=vv, in1=bks[2].t[:], op=ALU.mult),
                                     [TM.b[3], bks[2].b[0]], [PP.b[c % CPH]])
                            for d_ in range(NCK):
                                wv, wb = wload(colpanel(wco, d_ * 128), "p (k c) -> p k c", c=128)
                                gv, gb = wload(colpanel(w_in, C_GC + d_ * 128), "p (k c) -> p k c", c=128)
                                yb, gk = bank(), bank()
                                for kc in range(CPH):
                                    mm(yb.t[:], wv[:, CPH * half + kc, :], PP.t[:, kc, :], kc == 0, kc == CPH - 1, [wb, PP.b[kc]], yb.b)
                                for kc in range(NCK):
                                    mm(gk.t[:], gv[:, kc, :], U.t[:, kc, :], kc == 0, kc == NCK - 1, [gb, U.b[kc]], gk.b)
                                s = d_ % 2
                                k.op(act, lambda h: h.activation(out=TM.t[:, s, :], in_=gk.t[:], func=AF.Sigmoid), gk.b, [TM.b[s]])
                                k.op(dve, lambda h: h.tensor_tensor(out=TM.t[:, s, :], in0=TM.t[:, s, :], in1=yb.t[:], op=ALU.mult),
                                     [TM.b[s], yb.b[0]], [TM.b[s]])
                                k.op(dve, lambda h: h.tensor_tensor(out=R(ACC.t[:, d_, :]), in0=ACC.t[:, d_, :], in1=TM.t[:, s, :], op=ALU.add),
                                     [ACC.b[d_], TM.b[s]], [ACC.b[d_]])

                        k.barrier()
                    for d_ in range(NCK):
                        wv, wb = wload(colpanel(wo, d_ * 128), "p (k c) -> p k c", c=128)
                        ob = bank()
                        for kc in range(NCK):
                            mm(ob.t[:], wv[:, kc, :], ACC.t[:, kc, :], kc == 0, kc == NCK - 1, [wb, ACC.b[kc]], ob.b)
                        k.op(dve, lambda h: h.scalar_tensor_tensor(out=H.t[:, d_, :], in0=ob.t[:], scalar=mcol(t_, 5, d_), in1=H.t[:, d_, :],
                                                                   op0=ALU.mult, op1=ALU.add), [ob.b[0], MODD.b[0], H.b[d_]], [H.b[d_]])
                    norm_mod(H, U, t_, 2, p2)
                    ffn(H, U, t_, 2, f2g, f2u, f2d)
                    k.barrier()
                k.barrier()
            with ExitStack() as ph:
                FW = k.sb(ph, [128, D])
                YT = k.sb(ph, [128, 2, D], 2)
                JK = k.sb(ph, [128, D])
                SS = k.sb(ph, [128, 2], 2)
                k.spdma(lambda h: h.dma_start(out=FW.t[:], in_=fnwd.partition_broadcast(128)), [], FW.b)
                for tt in range(NT):
                    s = tt % 2
                    for q4 in range(4):
                        bk = bank()
                        for i in range(4):
                            c = 4 * q4 + i
                            k.op(pe, lambda h: h.transpose(bk.t[:, i * 128:(i + 1) * 128], H.t[:, c, tt * 128:(tt + 1) * 128], ident),
                                 [H.b[c], cB], bk.b, inc=(i == 3))
                        k.op(act, lambda h: h.activation(out=YT.t[:, s, q4 * 512:(q4 + 1) * 512], in_=bk.t[:], func=AF.Identity),
                             bk.b, [YT.b[s]])
                    k.op(dve, lambda h: h.tensor_tensor(out=JK.t[:], in0=YT.t[:, s, :], in1=YT.t[:, s, :], op=ALU.mult), [YT.b[s]], JK.b)
                    k.op(dve, lambda h: h.reduce_sum(out=SS.t[:, s:s + 1], in_=JK.t[:], axis=mybir.AxisListType.X), JK.b, [SS.b[s]])
                    k.op(dve, lambda h: h.tensor_scalar(out=SS.t[:, s:s + 1], in0=SS.t[:, s:s + 1], scalar1=1.0 / D, scalar2=EPS,
                                                        op0=ALU.mult, op1=ALU.add), [SS.b[s]], [SS.b[s]])
                    k.op(act, lambda h: h.activation(out=SS.t[:, s:s + 1], in_=SS.t[:, s:s + 1], func=AF.Sqrt), [SS.b[s]], [SS.b[s]])
                    k.op(dve, lambda h: h.reciprocal(out=SS.t[:, s:s + 1], in_=SS.t[:, s:s + 1]), [SS.b[s]], [SS.b[s]])
                    k.op(dve, lambda h: h.scalar_tensor_tensor(out=YT.t[:, s, :], in0=YT.t[:, s, :], scalar=SS.t[:, s:s + 1], in1=FW.t[:],
                                                               op0=ALU.mult, op1=ALU.mult), [YT.b[s], SS.b[s], FW.b[0]], [YT.b[s]])
                    k.spdma(lambda h: h.dma_start(out=y_out[tt * 128:(tt + 1) * 128, :], in_=YT.t[:, s, :]), [YT.b[s]], [Buf()])
                k.barrier()

        if stage >= 1:
            front(0, H0, 1, True)
            tap("h1_s", H0.t[:].rearrange("p a b -> p (a b)"), H0.b)
            tap("dt_s", DT.t[:, 0].rearrange("p a b -> p (a b)"), DT.b)
        if stage >= 2:
            with ExitStack() as ph:
                HQ = k.sb(ph, [128, NCK, T], NCK)
                for q in (1, 2, 3):
                    front(q, HQ, 1, False)
                k.barrier()
        if stage >= 3:
            state_pass()
            tap("hf", HF.t[:].rearrange("p a b -> p (a b)"), HF.b)
            tap("hb", HB.t[:].rearrange("p a b -> p (a b)"), HB.b)
        if stage >= 4:
            back(0, H0, 1, True, y_s)
        if stage >= 5:
            front(4, H0, 0, True)
            back(4, H0, 0, False, y_p)
        e = k.sp
        for s in k.lanes + [x.sig for x in k.engs if x is not e]:
            if s.count > e.seen.get(s, 0):
                e.h.wait_ge(s.sem, s.count)
                e.seen[s] = s.count
    return nc, dbg_out


def _fm(v, n):
    return np.ascontiguousarray(np.asarray(v, np.float32).reshape(n, 128).T)


def _consts():
    i = np.arange(128)
    k_, s_ = i[:, None], i[None, :]
    mats = [np.eye(128), np.ones((128, 128)), k_ <= s_, k_ > s_, k_ >= s_, k_ < s_]
    return np.ascontiguousarray(np.stack([m.astype(np.float32) for m in mats], 1).reshape(128, 6 * 128))


def make_inputs(core, I, shared):
    b, q = core // 4, core % 4
    xsam = I["x_sample"][b]
    order = [(q + i) % 4 for i in range(4)]
    blocks = [xsam[512 * s:512 * (s + 1)] for s in order]
    blocks.append(I["x_prompt"][2 * core:2 * core + 2].reshape(512, D))
    xs = np.ascontiguousarray(np.stack(blocks, 0), dtype=np.float32)
    cvv = np.stack([I["c_ctx"], I["c"][b]], 0).astype(np.float32)
    cv = np.ascontiguousarray(cvv.reshape(2, 16, 128).transpose(2, 1, 0).reshape(128, 32))
    h0 = np.ascontiguousarray(np.stack([I["state_ssm_fwd"][b, 0].reshape(2048, 128),
                                        I["state_ssm_bwd"][b, 0].reshape(2048, 128)], 0), dtype=np.float32)
    par = np.zeros((128, NPAR), np.float32)
    par[:, P_BADA:P_BADA + 144] = _fm(I["b_ada"][0], 144)
    par[:, P_N1:P_N1 + 16] = _fm(I["norm1_w"][0], 16)
    par[:, P_N2:P_N2 + 16] = _fm(I["norm2_w"][0], 16)
    par[:, P_N3:P_N3 + 16] = _fm(I["norm3_w"][0], 16)
    par[:, P_SN:P_SN + 16] = _fm(I["ssm_norm_w"][0], 16)
    cw = np.stack([_fm(I["conv_w"][0, i], 16) for i in range(3)], 2)
    par[:, P_CW:P_CW + 48] = cw.reshape(128, 48)
    scw = np.stack([_fm(I["ssm_conv_w"][0, i], 24) for i in range(3)], 2)
    par[:, P_SCW:P_SCW + 72] = scw.reshape(128, 72)
    par[:, P_SCB:P_SCB + 24] = _fm(I["ssm_conv_b"][0], 24)
    par[:, P_DTB:P_DTB + 64] = np.concatenate([I["dt_bias_f"][0], I["dt_bias_b"][0]])[None, :]
    par[:, P_ALOG:P_ALOG + 64] = np.concatenate([I["a_log_f"][0], I["a_log_b"][0]])[None, :]
    par[:, P_DSK:P_DSK + 32] = np.asarray(I["d_skip"][0])[None, :]
    m = np.zeros(24, np.float32)
    for i in range(4):
        true = order[i]
        m[i] = 0.0 if true == 0 else 1.0
        m[4 + i] = 0.0 if true == 3 else 1.0
    for si, seg in enumerate([1, 2, 3]):
        w = 1.0 if order[seg] == 0 else 0.0
        m[8 + si], m[12 + si] = w, 1.0 - w
    w = 1.0 if q == 0 else 0.0
    m[8 + 3], m[12 + 3] = w, 1.0 - w
    for si, seg in enumerate([3, 2, 1]):
        w = 1.0 if order[seg] == 3 else 0.0
        m[16 + si], m[20 + si] = w, 1.0 - w
    w = 1.0 if q == 3 else 0.0
    m[16 + 3], m[20 + 3] = w, 1.0 - w
    par[:, P_MSK:P_MSK + 24] = m[None, :]
    d = {"xs": xs, "cv": cv, "h0": h0, "par": par, "cst": _consts()}
    d.update(shared)
    return d


def _tile_cols(w):
    w = np.asarray(w, np.float32)
    K, C = w.shape
    return np.ascontiguousarray(w.reshape(K // 128, 128, C // 128, 128).transpose(2, 1, 0, 3).reshape(C // 128, 128, K))


def make_shared(I):
    f32 = lambda a: np.ascontiguousarray(a, dtype=np.float32)
    win = np.asarray(I["w_in"][0], np.float32)
    w_in_t = np.concatenate([_tile_cols(win[:, :C_DT]), _tile_cols(win[:, C_GC:])], 0)
    w_dt = np.ascontiguousarray(win[:, C_DT:C_GC].reshape(16, 128, 64).transpose(1, 0, 2).reshape(128, 1024))
    return {
        "fnw": f32(I["final_norm_w"]),
        "w_ada": _tile_cols(I["w_ada"][0]), "f1g": _tile_cols(I["ffn1_w_gate"][0]), "f1u": _tile_cols(I["ffn1_w_up"][0]),
        "f1d": f32(I["ffn1_w_down"][0]), "f2g": _tile_cols(I["ffn2_w_gate"][0]), "f2u": _tile_cols(I["ffn2_w_up"][0]),
        "f2d": f32(I["ffn2_w_down"][0]), "w_in": w_in_t, "w_dt": w_dt, "wco": _tile_cols(I["w_conv_out"][0]),
        "wso": f32(I["w_ssm_out"][0]), "wo": _tile_cols(I["w_o"][0]),
    }


def kernel(**inputs):
    I = {k_: np.asarray(v) for k_, v in inputs.items()}
    nc, _ = build()
    shared = make_shared(I)
    in_maps = [make_inputs(c, I, shared) for c in range(8)]
    res = run_bass_kernel_spmd(nc, in_maps, core_ids=list(range(8)))
    R_ = res.results
    y_prompt = np.zeros((16, 256, D), np.float32)
    y_sample = np.zeros((2, 2048, D), np.float32)
    nf = np.zeros((16, 1, 32, 64, 128), np.float32)
    nb = np.zeros((16, 1, 32, 64, 128), np.float32)
    for c in range(8):
        b, q = c // 4, c % 4
        y_prompt[2 * c:2 * c + 2] = np.asarray(R_[c]["y_p"]).reshape(2, 256, D)
        y_sample[b, 512 * q:512 * (q + 1)] = np.asarray(R_[c]["y_s"])
        nf[2 * c:2 * c + 2, 0] = np.asarray(R_[c]["st_f"]).reshape(2, 32, 64, 128)
        nb[2 * c:2 * c + 2, 0] = np.asarray(R_[c]["st_b"]).reshape(2, 32, 64, 128)
    return (y_prompt, y_sample, nf, nb)
```

```python
import numpy as np
from contextlib import ExitStack
import concourse.bass as bass
import concourse.mybir as mybir
from concourse.bass_utils import run_bass_kernel_spmd

F32 = mybir.dt.float32
F32R = mybir.dt.float32r
AF = mybir.ActivationFunctionType
ALU = mybir.AluOpType

D = 2048
DFF = 5632
T = 512
NT = 4
NCK = 16
EPS = 1e-6
C_CB, C_CC, C_CX, C_Z, C_XBC, C_DT, C_GC, C_GS = 0, 2048, 4096, 6144, 8192, 11264, 11328, 13376
NSLOT = 4
NMOD1 = 80
SUB = 99
VERBOSE = False

P_BADA = 0
P_N1 = P_BADA + 144
P_N2 = P_N1 + 16
P_N3 = P_N2 + 16
P_SN = P_N3 + 16
P_CW = P_SN + 16
P_SCW = P_CW + 48
P_SCB = P_SCW + 72
P_DTB = P_SCB + 24
P_ALOG = P_DTB + 64
P_DSK = P_ALOG + 64
P_MSK = P_DSK + 32
NPAR = P_MSK + 24


class Buf:
    __slots__ = ("lw", "rd")

    def __init__(self):
        self.lw = None
        self.rd = {}


class Sig:
    def __init__(self, sem, step):
        self.sem = sem
        self.step = step
        self.count = 0
        self.is_dma = step == 16


class Eng:
    def __init__(self, h, sig, skip_self=False):
        self.h = h
        self.sig = sig
        self.seen = {}
        self.skip_self = skip_self


class Tl:
    def __init__(self, t, nb):
        self.t = t
        self.b = [Buf() for _ in range(nb)]


class K:
    def __init__(self, nc, es):
        self.nc = nc
        self.es = es
        self.nsem = 0
        self.pe = Eng(nc.tensor, self.new_sig(1), True)
        self.act = Eng(nc.scalar, self.new_sig(1))
        self.dve = Eng(nc.vector, self.new_sig(1))
        self.pool = Eng(nc.gpsimd, self.new_sig(1))
        self.sp = Eng(nc.sync, self.new_sig(1))
        self.engs = [self.pe, self.act, self.dve, self.pool, self.sp]
        self.lanes = []
        self.nm = 0
        self.sp_lanes = [self.lane() for _ in range(8)]
        self.sp_i = 0

    def new_sig(self, step):
        sem = self.es.enter_context(self.nc.semaphore(f"sem{self.nsem}"))
        self.nsem += 1
        return Sig(sem, step)

    def lane(self):
        s = self.new_sig(16)
        self.lanes.append(s)
        return s

    def sb(self, es, shape, nb=1, dtype=F32):
        self.nm += 1
        return Tl(es.enter_context(self.nc.sbuf_tensor(f"t{self.nm}", list(shape), dtype)), nb)

    def op(self, eng, fn, reads=(), writes=(), sig=None, inc=True):
        deps = {}
        for b in reads:
            if b.lw is not None and deps.get(b.lw[0], 0) < b.lw[1]:
                deps[b.lw[0]] = b.lw[1]
        for b in writes:
            if b.lw is not None and deps.get(b.lw[0], 0) < b.lw[1]:
                deps[b.lw[0]] = b.lw[1]
            for s, v in b.rd.items():
                if deps.get(s, 0) < v:
                    deps[s] = v
        for s, v in deps.items():
            if s.is_dma:
                v = s.count
            if s is eng.sig and eng.skip_self:
                continue
            if eng.seen.get(s, 0) >= v:
                continue
            eng.h.wait_ge(s.sem, v)
            eng.seen[s] = v
        ins = fn(eng.h)
        sig = sig or eng.sig
        if inc:
            ins.then_inc(sig.sem, sig.step)
            sig.count += sig.step
            tick = sig.count
        else:
            tick = sig.count + sig.step
        for b in writes:
            b.lw = (sig, tick)
            b.rd = {}
        for b in reads:
            if b.rd.get(sig, 0) < tick:
                b.rd[sig] = tick
        return ins

    def spdma(self, fn, reads, writes):
        lane = self.sp_lanes[self.sp_i % len(self.sp_lanes)]
        self.sp_i += 1
        if lane.count > self.sp.seen.get(lane, 0):
            self.sp.h.wait_ge(lane.sem, lane.count)
            self.sp.seen[lane] = lane.count
        return self.op(self.sp, fn, reads, writes, sig=lane)

    def barrier(self, only=None):
        sigs = [e.sig for e in self.engs] + self.lanes
        for e in (only or (self.act, self.dve, self.sp)):
            for s in sigs:
                if s is e.sig:
                    continue
                if s.count > e.seen.get(s, 0):
                    e.h.wait_ge(s.sem, s.count)
                    e.seen[s] = s.count


def build(debug=(), stage=99):
    nc = bass.Bass("TRN2", target_bir_lowering=False)

    def din(name, shape):
        return nc.dram_tensor(name, list(shape), F32, kind="ExternalInput").ap()

    def dout(name, shape):
        return nc.dram_tensor(name, list(shape), F32, kind="ExternalOutput").ap()

    xs = din("xs", [5, T, D])
    cvd = din("cv", [128, 32])
    h0d = din("h0", [2, 2048, 128])
    pard = din("par", [128, NPAR])
    cstd = din("cst", [128, 6 * 128])
    fnwd = din("fnw", [D])
    w_ada = din("w_ada", [144, 128, 2048])
    f1g, f1u, f1d = din("f1g", [44, 128, 2048]), din("f1u", [44, 128, 2048]), din("f1d", [DFF, D])
    f2g, f2u, f2d = din("f2g", [44, 128, 2048]), din("f2u", [44, 128, 2048]), din("f2d", [DFF, D])
    w_in = din("w_in", [120, 128, 2048])
    w_dt = din("w_dt", [128, 1024])
    wco, wso, wo = din("wco", [16, 128, 2048]), din("wso", [D, D]), din("wo", [16, 128, 2048])
    y_s, y_p = dout("y_s", [T, D]), dout("y_p", [T, D])
    st_f, st_b = dout("st_f", [2, 2048, 128]), dout("st_b", [2, 2048, 128])
    scr = nc.dram_tensor("scr", [5, 40, 128, T], F32, kind="Internal").ap()
    dbg_out = {}

    with ExitStack() as es:
        k = K(nc, es)
        pe, act, dve, pool, sp = k.pe, k.act, k.dve, k.pool, k.sp

        CST = k.sb(es, [128, 6, 128])
        PAR = k.sb(es, [128, NPAR])
        MODD = k.sb(es, [128, 2, 9, 16])
        SCV = k.sb(es, [128, 16, 2])
        H0 = k.sb(es, [128, NCK, T], NCK)
        DT = k.sb(es, [128, 5, NT, 64], 5)
        AA = k.sb(es, [128, 5, NT, 64], 5)
        ANEG = k.sb(es, [128, 64])
        EDGE = k.sb(es, [128, 4, 24, 2], 4)
        HF = k.sb(es, [128, 4, T], 4)
        HB = k.sb(es, [128, 4, T], 4)
        banks = [Tl(es.enter_context(nc.psum_tensor(f"bank{i}", [128, 512], F32)), 1) for i in range(8)]
        slots = [k.sb(es, [128, 2048]) for _ in range(NSLOT)]
        slanes = [k.lane() for _ in range(NSLOT)]
        st = {"bank": 0, "slot": 0}
        ident, ones = CST.t[:, 0, :], CST.t[:, 1, :]
        Tle, Tgt, Tge, Tlt = (CST.t[:, i, :] for i in (2, 3, 4, 5))
        cB = CST.b[0]
        pB = PAR.b[0]

        def bank():
            i = st["bank"]
            st["bank"] = (i + 1) % 7
            return banks[i]
        stat_bank = banks[7]

        def R(ap):
            return ap.bitcast(F32R)

        def wload(src, pat=None, **kw):
            i = st["slot"] % len(slots)
            st["slot"] = (i + 1) % len(slots)
            if i >= NSLOT and st.get("fence"):
                k.barrier(only=(pool,))
                st["fence"] = False
            s = slots[i]
            n = 1
            for d_ in src.shape[1:]:
                n *= d_
            v = s.t[:, 0:n]
            if pat is not None:
                v = v.rearrange(pat, **kw)
            k.op(pool, lambda h: h.dma_start(out=R(v), in_=src), [], [s.b[0]], sig=slanes[i])
            return v, s.b[0]

        class extra_slots:
            def __init__(self, ph, n):
                st["fence"] = True
                st["slot"] = 0
                n = min(n, nc.sbuf_bytes_remaining // 8192)
                for _ in range(n):
                    slots.append(k.sb(ph, [128, 2048]))
                    slanes.append(k.lane())
                self.n = n

            def close(self):
                for _ in range(self.n):
                    slots.pop()
                    slanes.pop()
                st["slot"] = 0
                st["fence"] = False

        def colpanel(w, c0):
            if w is w_in:
                j = c0 // 128 if c0 < C_DT else 88 + (c0 - C_GC) // 128
            else:
                j = c0 // 128
            return w[j]

        def mm(out, lhsT, rhs, start, stop, reads, writes, r=True, inc=None):
            if r:
                lhsT, rhs = R(lhsT), R(rhs)
            return k.op(pe, lambda h: h.matmul(out, lhsT=lhsT, rhs=rhs, start=start, stop=stop),
                        reads, writes, inc=(stop if inc is None else inc))

        def tap(name, tl_ap, bufs):
            if name in debug:
                shp = list(tl_ap.shape)
                o = dout("dbg_" + name, shp)
                dbg_out[name] = shp
                k.spdma(lambda h: h.dma_start(out=o, in_=tl_ap), bufs, [Buf()])

        k.op(pool, lambda h: h.dma_start(out=R(CST.t[:].rearrange("p a b -> p (a b)")), in_=cstd), [], [cB], sig=k.lane())
        k.spdma(lambda h: h.dma_start(out=PAR.t[:], in_=pard), [], [pB])
        k.op(act, lambda h: h.activation(out=ANEG.t[:], in_=PAR.t[:, P_ALOG:P_ALOG + 64], func=AF.Exp), [pB], ANEG.b)
        k.op(dve, lambda h: h.tensor_scalar(out=ANEG.t[:], in0=ANEG.t[:], scalar1=-1.0, scalar2=None, op0=ALU.mult),
             ANEG.b, ANEG.b)

        with ExitStack() as ph:
            CV = k.sb(ph, [128, 16, 2])
            MOD = k.sb(ph, [128, NMOD1, 2])
            k.spdma(lambda h: h.dma_start(out=CV.t[:].rearrange("p a b -> p (a b)"), in_=cvd), [], CV.b)
            k.op(act, lambda h: h.activation(out=R(SCV.t[:]), in_=CV.t[:], func=AF.Silu), CV.b, SCV.b)
            mb = bank()
            for j in range(NMOD1):
                wv, wb = wload(colpanel(w_ada, j * 128), "p (k c) -> p k c", c=128)
                for kc in range(NCK):
                    mm(mb.t[:, 2 * j:2 * j + 2], wv[:, kc, :], SCV.t[:, kc, :], kc == 0, kc == NCK - 1,
                       [wb, SCV.b[0]], mb.b)
            k.op(dve, lambda h: h.tensor_tensor(
                out=MOD.t[:], in0=mb.t[:, 0:2 * NMOD1].rearrange("p (j t) -> p j t", t=2),
                in1=PAR.t[:, P_BADA:P_BADA + NMOD1].unsqueeze(2).to_broadcast([128, NMOD1, 2]), op=ALU.add),
                [mb.b[0], pB], MOD.b)
            for t_ in range(2):
                for n_, pn in enumerate((P_N1, P_N2)):
                    base = 48 * n_
                    k.op(dve, lambda h: h.scalar_tensor_tensor(
                        out=MODD.t[:, t_, 3 * n_, :], in0=MOD.t[:, base + 16:base + 32, t_], scalar=1.0,
                        in1=PAR.t[:, pn:pn + 16], op0=ALU.add, op1=ALU.mult), [MOD.b[0], pB], MODD.b)
                    k.op(dve, lambda h: h.tensor_copy(out=MODD.t[:, t_, 3 * n_ + 1, :], in_=MOD.t[:, base:base + 16, t_]),
                         MOD.b, MODD.b)
                    if n_ == 0:
                        k.op(dve, lambda h: h.tensor_scalar(
                            out=MODD.t[:, t_, 3 * n_ + 2, :], in0=MOD.t[:, base + 32:base + 48, t_],
                            scalar1=0.5, scalar2=None, op0=ALU.mult), MOD.b, MODD.b)
            k.barrier()
        tap("modd", MODD.t[:].rearrange("p a b c -> p (a b c)"), MODD.b)

        def mcol(t_, kind, c):
            return MODD.t[:, t_, kind, c:c + 1]

        g2 = {"next": NMOD1, "loaded": []}

        def gemv2_step():
            for (j, wv, wb) in g2["loaded"]:
                c0 = 2 * (j - NMOD1)
                for kc in range(NCK):
                    mm(stat_bank.t[:, c0:c0 + 2], wv[:, kc, :], SCV.t[:, kc, :], kc == 0, kc == NCK - 1,
                       [wb, SCV.b[0]], stat_bank.b)
            g2["loaded"] = []
            for _ in range(4):
                j = g2["next"]
                if j >= 144:
                    break
                g2["next"] += 1
                wv, wb = wload(colpanel(w_ada, j * 128), "p (k c) -> p k c", c=128)
                g2["loaded"].append((j, wv, wb))

        def gemv2_finish():
            while g2["loaded"] or g2["next"] < 144:
                gemv2_step()
            ps3 = stat_bank.t[:, 0:2 * (144 - NMOD1)].rearrange("p (j t) -> p j t", t=2)
            for t_ in range(2):
                bb = lambda a: PAR.t[:, P_BADA + NMOD1 + a:P_BADA + NMOD1 + a + 16]
                k.op(dve, lambda h: h.tensor_tensor(out=MODD.t[:, t_, 5, :], in0=ps3[:, 0:16, t_], in1=bb(0), op=ALU.add),
                     [stat_bank.b[0], pB], MODD.b)
                k.op(dve, lambda h: h.tensor_tensor(out=MODD.t[:, t_, 7, :], in0=ps3[:, 16:32, t_], in1=bb(16), op=ALU.add),
                     [stat_bank.b[0], pB], MODD.b)
                k.op(dve, lambda h: h.tensor_tensor(out=MODD.t[:, t_, 6, :], in0=ps3[:, 32:48, t_], in1=bb(32), op=ALU.add),
                     [stat_bank.b[0], pB], MODD.b)
                k.op(dve, lambda h: h.scalar_tensor_tensor(out=MODD.t[:, t_, 6, :], in0=MODD.t[:, t_, 6, :], scalar=1.0,
                                                           in1=PAR.t[:, P_N3:P_N3 + 16], op0=ALU.add, op1=ALU.mult), MODD.b + [pB], MODD.b)
                k.op(dve, lambda h: h.tensor_tensor(out=MODD.t[:, t_, 8, :], in0=ps3[:, 48:64, t_], in1=bb(48), op=ALU.add),
                     [stat_bank.b[0], pB], MODD.b)
                k.op(dve, lambda h: h.tensor_scalar(out=MODD.t[:, t_, 8, :], in0=MODD.t[:, t_, 8, :], scalar1=0.5, scalar2=None,
                                                    op0=ALU.mult), MODD.b, MODD.b)

        def load_x(blk, H):
            with ExitStack() as ph:
                XT = k.sb(ph, [128, NT, D], NT)
                for tt in range(NT):
                    k.spdma(lambda h: h.dma_start(out=XT.t[:, tt, :], in_=xs[blk, tt * 128:(tt + 1) * 128, :]),
                         [], [XT.b[tt]])
                n_ev = 0
                for tt in range(NT):
                    for q4 in range(4):
                        bk = bank()
                        for i in range(4):
                            c = 4 * q4 + i
                            k.op(pe, lambda h: h.transpose(bk.t[:, i * 128:(i + 1) * 128], XT.t[:, tt, c * 128:(c + 1) * 128], ident),
                                 [XT.b[tt], cB], bk.b, inc=(i == 3))
                        dst = H.t[:, 4 * q4:4 * q4 + 4, tt * 128:(tt + 1) * 128]
                        src = bk.t[:].rearrange("p (a b) -> p a b", b=128)
                        hb = [H.b[4 * q4 + i] for i in range(4)]
                        if n_ev % 2 == 0:
                            k.op(act, lambda h: h.activation(out=dst, in_=src, func=AF.Identity), bk.b, hb)
                        else:
                            k.op(dve, lambda h: h.tensor_copy(out=dst, in_=src), bk.b, hb)
                        n_ev += 1
                k.barrier()

        def rms_stats(src_fn, nchunk, reads_fn, RSTD, TMP):
            nb_ = len(TMP.b)
            for c in range(nchunk):
                tb = TMP.b[c % nb_]
                if c % 2 == 0:
                    k.op(act, lambda h: h.activation(out=R(TMP.t[:, c % nb_, :]), in_=src_fn(c), func=AF.Square),
                         reads_fn(c), [tb])
                else:
                    k.op(dve, lambda h: h.tensor_tensor(out=R(TMP.t[:, c % nb_, :]), in0=src_fn(c), in1=src_fn(c), op=ALU.mult),
                         reads_fn(c), [tb])
                mm(stat_bank.t[:], ones, TMP.t[:, c % nb_, :], c == 0, c == nchunk - 1, [cB, tb], stat_bank.b, inc=True)
            k.op(dve, lambda h: h.tensor_scalar(out=RSTD.t[:], in0=stat_bank.t[:], scalar1=1.0 / D, scalar2=EPS,
                                                op0=ALU.mult, op1=ALU.add), stat_bank.b, RSTD.b)
            k.op(act, lambda h: h.activation(out=RSTD.t[:], in_=RSTD.t[:], func=AF.Sqrt), RSTD.b, RSTD.b)
            k.op(dve, lambda h: h.reciprocal(out=RSTD.t[:], in_=RSTD.t[:]), RSTD.b, RSTD.b)

        def norm_mod(H, U, t_, n_, ph):
            with ExitStack() as p2:
                RSTD = k.sb(p2, [128, T])
                TMP = k.sb(p2, [128, 2, T], 2)
                SQT = k.sb(p2, [128, 4, T], 4)
                rms_stats(lambda c: H.t[:, c, :], NCK, lambda c: [H.b[c]], RSTD, SQT)
                for c in range(NCK):
                    tb = TMP.b[c % 2]
                    k.op(dve, lambda h: h.scalar_tensor_tensor(
                        out=TMP.t[:, c % 2, :], in0=H.t[:, c, :], scalar=mcol(t_, 3 * n_, c), in1=RSTD.t[:],
                        op0=ALU.mult, op1=ALU.mult), [H.b[c], MODD.b[0], RSTD.b[0]], [tb])
                    k.op(act, lambda h: h.activation(out=R(U.t[:, c, :]), in_=TMP.t[:, c % 2, :], func=AF.Identity,
                                                     bias=mcol(t_, 3 * n_ + 1, c), scale=1.0),
                         [tb, MODD.b[0]], [U.b[c]])
                k.barrier()

        def ffn(H, U, t_, n_, wg, wu, wd):
            G = 2
            with ExitStack() as ph:
                HID = k.sb(ph, [128, 2 * G, T], 2 * G)
                SG = k.sb(ph, [128, 2, T], 2)
                xs_ = extra_slots(ph, 4)
                for fg in range(DFF // 128 // G):
                    wds = []
                    for i in range(G):
                        f = fg * G + i
                        hs = (fg % 2) * G + i
                        gv, gb = wload(colpanel(wg, f * 128), "p (k c) -> p k c", c=128)
                        uv, ub = wload(colpanel(wu, f * 128), "p (k c) -> p k c", c=128)
                        wds.append(wload(wd[f * 128:(f + 1) * 128, :]))
                        gbk, ubk = bank(), bank()
                        for kc in range(NCK):
                            mm(gbk.t[:], gv[:, kc, :], U.t[:, kc, :], kc == 0, kc == NCK - 1, [gb, U.b[kc]], gbk.b)
                        for kc in range(NCK):
                            mm(ubk.t[:], uv[:, kc, :], U.t[:, kc, :], kc == 0, kc == NCK - 1, [ub, U.b[kc]], ubk.b)
                        k.op(act, lambda h: h.activation(out=SG.t[:, i, :], in_=gbk.t[:], func=AF.Silu), gbk.b, [SG.b[i]])
                        k.op(dve, lambda h: h.tensor_tensor(out=R(HID.t[:, hs, :]), in0=SG.t[:, i, :], in1=ubk.t[:], op=ALU.mult),
                             [SG.b[i], ubk.b[0]], [HID.b[hs]])
                    for d_ in range(NCK):
                        ob = bank()
                        for i in range(G):
                            hs = (fg % 2) * G + i
                            mm(ob.t[:], wds[i][0][:, d_ * 128:(d_ + 1) * 128], HID.t[:, hs, :], i == 0, i == G - 1,
                               [wds[i][1], HID.b[hs]], ob.b)
                        k.op(dve, lambda h: h.scalar_tensor_tensor(
                            out=H.t[:, d_, :], in0=ob.t[:], scalar=mcol(t_, 3 * n_ + 2, d_), in1=H.t[:, d_, :],
                            op0=ALU.mult, op1=ALU.add), [ob.b[0], MODD.b[0], H.b[d_]], [H.b[d_]])
                xs_.close()
                k.barrier()

        def proj_spill(blk, U, chunks):
            with ExitStack() as ph:
                STG = k.sb(ph, [128, 3, T], 3)
                for n, (col, slot_idx) in enumerate(chunks):
                    wv, wb = wload(colpanel(w_in, col), "p (k c) -> p k c", c=128)
                    bk = bank()
                    for kc in range(NCK):
                        mm(bk.t[:], wv[:, kc, :], U.t[:, kc, :], kc == 0, kc == NCK - 1, [wb, U.b[kc]], bk.b)
                    s = n % 3
                    k.op(act, lambda h: h.activation(out=STG.t[:, s, :], in_=bk.t[:], func=AF.Identity), bk.b, [STG.b[s]])
                    if blk < 4 and slot_idx < 24:
                        k.op(dve, lambda h: h.tensor_copy(out=EDGE.t[:, blk, slot_idx, :], in_=STG.t[:, s, 0:T:T - 1]),
                             [STG.b[s]], [EDGE.b[blk]])
                    k.spdma(lambda h: h.dma_start(out=scr[blk, slot_idx], in_=STG.t[:, s, :]), [STG.b[s]], [scrB[blk][slot_idx]])
                k.barrier()

        scrB = [[Buf() for _ in range(40)] for _ in range(5)]

        def proj_dt(blk, U):
            with ExitStack() as ph:
                TMPD = k.sb(ph, [128, NT, 64])
                wdv, wdb = wload(w_dt, "p (k c) -> p k c", c=64)
                bk = bank()
                for tt in range(NT):
                    for kc in range(NCK):
                        mm(bk.t[:, tt * 64:(tt + 1) * 64], U.t[:, kc, tt * 128:(tt + 1) * 128], wdv[:, kc, :],
                           kc == 0, kc == NCK - 1, [U.b[kc], wdb], bk.b)
                k.op(dve, lambda h: h.tensor_tensor(
                    out=TMPD.t[:], in0=bk.t[:, 0:NT * 64].rearrange("p (t c) -> p t c", c=64),
                    in1=PAR.t[:, P_DTB:P_DTB + 64].unsqueeze(1).to_broadcast([128, NT, 64]), op=ALU.add),
                    [bk.b[0], pB], TMPD.b)
                k.op(act, lambda h: h.activation(out=TMPD.t[:], in_=TMPD.t[:], func=AF.Exp), TMPD.b, TMPD.b)
                k.op(act, lambda h: h.activation(out=DT.t[:, blk], in_=TMPD.t[:], func=AF.Ln, bias=1.0, scale=1.0),
                     TMPD.b, [DT.b[blk]])
                k.op(dve, lambda h: h.tensor_tensor(
                    out=AA.t[:, blk], in0=DT.t[:, blk], in1=ANEG.t[:].unsqueeze(1).to_broadcast([128, NT, 64]), op=ALU.mult),
                    [DT.b[blk], ANEG.b[0]], [AA.b[blk]])
                k.barrier()


        def front(blk, H, t_, with_z):
            with ExitStack() as ph:
                U = k.sb(ph, [128, NCK, T], NCK)
                load_x(blk, H)
                if SUB >= 2:
                    norm_mod(H, U, t_, 0, ph)
                if SUB >= 3:
                    ffn(H, U, t_, 0, f1g, f1u, f1d)
                if SUB >= 4:
                    norm_mod(H, U, t_, 1, ph)
                chunks = [(C_XBC + j * 128, j) for j in range(24)]
                if with_z:
                    chunks += [(C_Z + j * 128, 24 + j) for j in range(16)]
                if SUB >= 5:
                    proj_spill(blk, U, chunks)
                if SUB >= 6:
                    proj_dt(blk, U)
                if SUB < 6:
                    tap("u", U.t[:].rearrange("p a b -> p (a b)"), U.b)
                k.barrier()

        def make_dec(blk, DEC):
            for tt in range(NT):
                bk = bank()
                for i, tri in enumerate((Tle, Tgt, Tge, Tlt, ones)):
                    mm(bk.t[:, i * 64:(i + 1) * 64], tri, AA.t[:, blk, tt, :], True, True, [cB, AA.b[blk]], bk.b,
                       r=False, inc=(i == 4))
                k.op(act, lambda h: h.activation(out=DEC.t[:, tt], in_=bk.t[:, 0:320].rearrange("p (a c) -> p a c", c=64),
                                                 func=AF.Exp), bk.b, DEC.b)

        def prep(blk, g, W, need_c, seqs, halo):
            XG, BC, CT_, XTOK, BTOK, PRE = W["XG"], W["BC"], W["CTMP"], W["XTOK"], W["BTOK"], W["PRE"]
            srcs = [(4 * g + i, XG.t[:, i, :], XG.b[i], True) for i in range(4)] + [(16 + g, BC.t[:, 0, :], BC.b[0], True)]
            if need_c:
                srcs.append((20 + g, BC.t[:, 1, :], BC.b[1], True))
            L = T // seqs
            nct = CT_.t.shape[1]
            for i2 in range(2):
                k.op(dve, lambda h: h.memset(PRE.t[:, i2, 0:seqs * (L + 2)].rearrange("p (s l) -> p s l", l=L + 2)[:, :, 0:L + 2:L + 1], 0.0),
                     [], [PRE.b[i2]])
            for n_src, (j, fin, fb, as_r) in enumerate(srcs):
                P3 = PRE.t[:, n_src % 2, 0:seqs * (L + 2)].rearrange("p (s l) -> p s l", l=L + 2)
                db = PRE.b[n_src % 2]
                k.spdma(lambda h: h.dma_start(out=P3[:, :, 1:L + 1], in_=scr[blk, j].rearrange("p (s l) -> p s l", l=L)),
                     [scrB[blk][j]], [db])
                w0 = PAR.t[:, P_SCW + 3 * j:P_SCW + 3 * j + 1]
                w1 = PAR.t[:, P_SCW + 3 * j + 1:P_SCW + 3 * j + 2]
                w2 = PAR.t[:, P_SCW + 3 * j + 2:P_SCW + 3 * j + 3]
                cbias = PAR.t[:, P_SCB + j:P_SCB + j + 1]
                acc = CT_.t[:, n_src % nct, :]
                ab = [CT_.b[n_src % nct]]
                a3 = acc.rearrange("p (s l) -> p s l", l=L)
                if halo is not None:
                    li, ri, ml, mr = halo
                    k.op(dve, lambda h: h.tensor_scalar(out=P3[:, 0, 0:1], in0=EDGE.t[:, li, j, 1:2], scalar1=ml, scalar2=None,
                                                        op0=ALU.mult), [EDGE.b[li], pB], [db])
                    k.op(dve, lambda h: h.tensor_scalar(out=P3[:, 0, L + 1:L + 2], in0=EDGE.t[:, ri, j, 0:1], scalar1=mr, scalar2=None,
                                                        op0=ALU.mult), [EDGE.b[ri], pB], [db])
                k.op(dve, lambda h: h.tensor_scalar(out=a3, in0=P3[:, :, 1:L + 1], scalar1=w1, scalar2=None, op0=ALU.mult), [db, pB], ab)
                k.op(dve, lambda h: h.scalar_tensor_tensor(out=a3, in0=P3[:, :, 0:L], scalar=w0, in1=a3, op0=ALU.mult, op1=ALU.add),
                     [db, pB] + ab, ab)
                k.op(dve, lambda h: h.scalar_tensor_tensor(out=a3, in0=P3[:, :, 2:L + 2], scalar=w2, in1=a3, op0=ALU.mult, op1=ALU.add),
                     [db, pB] + ab, ab)
                k.op(act, lambda h: h.activation(out=(R(fin) if as_r else fin), in_=acc, func=AF.Silu, bias=cbias, scale=1.0),
                     ab + [pB], [fb])
            for tt in range(NT):
                bk = bank()
                for i in range(4):
                    k.op(pe, lambda h: h.transpose(bk.t[:, i * 128:(i + 1) * 128], XG.t[:, i, tt * 128:(tt + 1) * 128], ident),
                         [XG.b[i], cB], bk.b, inc=(i == 3))
                k.op(act, lambda h: h.activation(out=XTOK.t[:, tt, :], in_=bk.t[:], func=AF.Identity), bk.b, [XTOK.b[tt]])
            bk = bank()
            for tt in range(NT):
                k.op(pe, lambda h: h.transpose(bk.t[:, tt * 128:(tt + 1) * 128], BC.t[:, 0, tt * 128:(tt + 1) * 128], ident),
                     [BC.b[0], cB], bk.b, inc=(tt == NT - 1))
            k.op(act, lambda h: h.activation(out=R(BTOK.t[:].rearrange("p a b -> p (a b)")), in_=bk.t[:], func=AF.Identity),
                 bk.b, BTOK.b)

        def ssd_A(blk, g, tt, dr, Wg, Wc, DEC):
            XTOK, BTOK, BC = Wg["XTOK"], Wg["BTOK"], Wg["BC"]
            XD, XDD, RM, LM, CBM = Wc["XD"], Wc["XDD"], Wc["RM"], Wc["LM"], Wc["CBM"]
            hs = slice(dr * 32 + g * 8, dr * 32 + g * 8 + 8)
            x3 = XTOK.t[:, tt, :].rearrange("p (h d) -> p h d", d=64)
            bc8 = lambda ap: ap.unsqueeze(2).to_broadcast([128, 8, 64])
            k.op(dve, lambda h: h.tensor_tensor(out=R(XD.t[:].rearrange("p (h d) -> p h d", d=64)), in0=x3,
                                                in1=bc8(DT.t[:, blk, tt, hs]), op=ALU.mult), [XTOK.b[tt], DT.b[blk]], XD.b)
            k.op(dve, lambda h: h.tensor_tensor(out=R(XDD.t[:].rearrange("p (h d) -> p h d", d=64)),
                                                in0=XD.t[:].rearrange("p (h d) -> p h d", d=64),
                                                in1=bc8(DEC.t[:, tt, 1 + 2 * dr, hs]), op=ALU.mult), XD.b + DEC.b, XDD.b)
            tri_l, tri_r = (Tgt, Tle) if dr == 0 else (Tlt, Tge)
            k.op(dve, lambda h: h.tensor_tensor(
                out=R(RM.t[:]), in0=AA.t[:, blk, tt, hs].unsqueeze(2).to_broadcast([128, 8, 128]),
                in1=tri_r.unsqueeze(1).to_broadcast([128, 8, 128]), op=ALU.mult), [AA.b[blk], cB], RM.b)
            for half in range(2):
                zb = bank()
                mm(zb.t[:], tri_l, RM.t[:, 4 * half:4 * half + 4, :].rearrange("p a b -> p (a b)"), True, True,
                   [cB, RM.b[0]], zb.b)
                k.op(act, lambda h: h.activation(out=R(LM.t[:, 4 * half:4 * half + 4, :].rearrange("p a b -> p (a b)")),
                                                 in_=zb.t[:], func=AF.Exp), zb.b, LM.b)
            cb_ = bank()
            mm(cb_.t[:, 0:128], BC.t[:, 0, tt * 128:(tt + 1) * 128], BC.t[:, 1, tt * 128:(tt + 1) * 128], True, True,
               [BC.b[0], BC.b[1]], cb_.b)
            tri_m = Tle if dr == 0 else Tge
            k.op(dve, lambda h: h.tensor_tensor(out=CBM.t[:], in0=cb_.t[:, 0:128], in1=tri_m, op=ALU.mult), [cb_.b[0], cB], CBM.b)
            return None

        def ssd_B(blk, g, tt, dr, Wg, Wc, DEC, S, s_zero, YACC, carry):
            BC, BTOK = Wg["BC"], Wg["BTOK"]
            XD, XDD, LM, CBM, FT = Wc["XD"], Wc["XDD"], Wc["LM"], Wc["CBM"], Wc["FT"]
            hs = slice(dr * 32 + g * 8, dr * 32 + g * 8 + 8)
            sap, sbuf = S
            bc8 = lambda ap: ap.unsqueeze(2).to_broadcast([128, 8, 64])
            if not s_zero:
                ob = bank()
                mm(ob.t[:], BC.t[:, 1, tt * 128:(tt + 1) * 128], sap, True, True, [BC.b[1], sbuf], ob.b)
            sb_ = bank()
            mm(sb_.t[:], BTOK.t[:, tt, :], XDD.t[:], True, True, [BTOK.b[0], XDD.b[0]], sb_.b)
            if s_zero:
                k.op(act, lambda h: h.activation(out=R(sap), in_=sb_.t[:], func=AF.Identity), sb_.b, [sbuf])
            else:
                k.op(dve, lambda h: h.tensor_tensor(out=FT.t[:].rearrange("p (h d) -> p h d", d=64),
                                                    in0=sap.rearrange("p (h d) -> p h d", d=64),
                                                    in1=bc8(DEC.t[:, tt, 4, hs]), op=ALU.mult), [sbuf] + DEC.b, FT.b)
                k.op(dve, lambda h: h.tensor_tensor(out=R(sap), in0=FT.t[:], in1=sb_.t[:], op=ALU.add), FT.b + [sb_.b[0]], [sbuf])
            k.op(dve, lambda h: h.tensor_tensor(out=R(LM.t[:]), in0=LM.t[:], in1=CBM.t[:].unsqueeze(1).to_broadcast([128, 8, 128]),
                                                op=ALU.mult), LM.b + CBM.b, LM.b)
            yb = bank()
            for hh in range(8):
                mm(yb.t[:, hh * 64:(hh + 1) * 64], LM.t[:, hh, :], XD.t[:, hh * 64:(hh + 1) * 64], True, True,
                   [LM.b[0], XD.b[0]], yb.b, inc=(hh == 7))
            k.op(dve, lambda h: h.tensor_tensor(out=YACC.t[:, tt, :], in0=YACC.t[:, tt, :], in1=yb.t[:], op=ALU.add),
                 [YACC.b[tt], yb.b[0]], [YACC.b[tt]])
            if not s_zero:
                k.op(dve, lambda h: h.tensor_tensor(out=FT.t[:].rearrange("p (h d) -> p h d", d=64),
                                                    in0=ob.t[:].rearrange("p (h d) -> p h d", d=64),
                                                    in1=bc8(DEC.t[:, tt, 2 * dr, hs]), op=ALU.mult),
                     [ob.b[0]] + DEC.b, FT.b)
                k.op(dve, lambda h: h.tensor_tensor(out=YACC.t[:, tt, :], in0=YACC.t[:, tt, :], in1=FT.t[:], op=ALU.add),
                     [YACC.b[tt]] + FT.b, [YACC.b[tt]])

        def ssd_run(blk, g, Wg, chk, DEC, YACC, items):
            n = len(items)
            carry = [None] * n
            skew = 1 if len(chk) > 1 else 0
            for i in range(n + skew):
                if i < n:
                    tt, dr, S, s_zero, post = items[i]
                    carry[i] = ssd_A(blk, g, tt, dr, Wg, chk[i % len(chk)], DEC)
                j = i - skew
                if j >= 0:
                    tt, dr, S, s_zero, post = items[j]
                    ssd_B(blk, g, tt, dr, Wg, chk[j % len(chk)], DEC, S, s_zero, YACC, carry[j])
                    if post is not None:
                        post()

        def alloc_work(ph, full, ng, ncb):
            grp = [{"XG": k.sb(ph, [128, 4, T], 4), "BC": k.sb(ph, [128, 2, T], 2), "CTMP": k.sb(ph, [128, 2, T], 2),
                    "PRE": k.sb(ph, [128, 2, T + 4], 2), "XTOK": k.sb(ph, [128, NT, T], NT), "BTOK": k.sb(ph, [128, NT, 128])}
                   for _ in range(ng)]
            chk = []
            for i in range(ncb):
                need = (3 * T + (2 * 8 * 128 + 128 if full else 0)) * 4
                if i > 0 and nc.sbuf_bytes_remaining < need:
                    break
                c = {"XD": k.sb(ph, [128, T]), "XDD": k.sb(ph, [128, T]), "FT": k.sb(ph, [128, T])}
                if full:
                    c.update({"RM": k.sb(ph, [128, 8, 128]), "LM": k.sb(ph, [128, 8, 128]), "CBM": k.sb(ph, [128, 128])})
                chk.append(c)
            return grp, chk

        def state_pass():
            with ExitStack() as ph:
                grp, _ = alloc_work(ph, False, 1, 0)
                g2 = dict(grp[0])
                g2["XTOK"], g2["BTOK"] = k.sb(ph, [128, NT, T], NT), k.sb(ph, [128, NT, 128])
                grp.append(g2)
                DEC = k.sb(ph, [128, NT, 5, 64])
                SPF = k.sb(ph, [128, NT, 64])
                MS = k.sb(ph, [128, NT, 64])
                DSEG = k.sb(ph, [128, 4, 64], 4)
                XDDS = k.sb(ph, [128, 4, T], 4)
                FT = k.sb(ph, [128, 2, T], 2)
                SSB = k.sb(ph, [128, 3, 4, T], 12)
                H0T = k.sb(ph, [128, 2, 4, T], 8)
                STG = k.sb(ph, [128, 4, 128])
                for dr in range(2):
                    for g in range(4):
                        k.spdma(lambda h: h.dma_start(out=STG.t[:], in_=h0d[dr, g * 512:(g + 1) * 512, :].rearrange("(a p) n -> p a n", p=128)),
                             [], STG.b)
                        bk = bank()
                        for i in range(4):
                            k.op(pe, lambda h: h.transpose(bk.t[:, i * 128:(i + 1) * 128], STG.t[:, i, :], ident),
                                 STG.b + [cB], bk.b, inc=(i == 3))
                        k.op(act, lambda h: h.activation(out=H0T.t[:, dr, g, :], in_=bk.t[:], func=AF.Identity), bk.b, [H0T.b[dr * 4 + g]])
                mk = lambda i: PAR.t[:, P_MSK + i:P_MSK + i + 1]
                bc8 = lambda ap: ap.unsqueeze(2).to_broadcast([128, 8, 64])
                cnt = {"x": 0, "f": 0, "g": 0}

                def step(HH, dr, g, idx, first, dseg_ap, dseg_b, add_ap, add_b):
                    wrap0, keep0 = (8, 12) if dr == 0 else (16, 20)
                    ft, fb = FT.t[:, cnt["f"] % 2, :], FT.b[cnt["f"] % 2]
                    cnt["f"] += 1
                    if first:
                        k.op(dve, lambda h: h.tensor_scalar(out=R(HH.t[:, g, :]), in0=H0T.t[:, dr, g, :], scalar1=mk(wrap0 + idx),
                                                            scalar2=None, op0=ALU.mult), [H0T.b[dr * 4 + g], pB], [HH.b[g]])
                    else:
                        k.op(dve, lambda h: h.tensor_scalar(out=ft, in0=HH.t[:, g, :], scalar1=mk(keep0 + idx),
                                                            scalar2=None, op0=ALU.mult), [HH.b[g], pB], [fb])
                        k.op(dve, lambda h: h.scalar_tensor_tensor(out=R(HH.t[:, g, :]), in0=H0T.t[:, dr, g, :], scalar=mk(wrap0 + idx),
                                                                   in1=ft, op0=ALU.mult, op1=ALU.add),
                             [H0T.b[dr * 4 + g], pB, fb], [HH.b[g]])
                    if add_ap is None:
                        return
                    k.op(dve, lambda h: h.tensor_tensor(out=ft.rearrange("p (h d) -> p h d", d=64),
                                                        in0=HH.t[:, g, :].rearrange("p (h d) -> p h d", d=64),
                                                        in1=bc8(dseg_ap), op=ALU.mult), [HH.b[g]] + dseg_b, [fb])
                    k.op(dve, lambda h: h.tensor_tensor(out=R(HH.t[:, g, :]), in0=ft, in1=add_ap, op=ALU.add), [fb] + add_b, [HH.b[g]])

                for si, seg in enumerate([1, 2, 3]):
                    make_dec(seg, DEC)
                    gemv2_step()
                    k.op(dve, lambda h: h.memset(SPF.t[:], 1.0), [], SPF.b)
                    for tt in (2, 1, 0):
                        k.op(dve, lambda h: h.tensor_tensor(out=SPF.t[:, tt, 0:32], in0=SPF.t[:, tt + 1, 0:32], in1=DEC.t[:, tt + 1, 4, 0:32], op=ALU.mult),
                             SPF.b + DEC.b, SPF.b)
                    for tt in (1, 2, 3):
                        k.op(dve, lambda h: h.tensor_tensor(out=SPF.t[:, tt, 32:64], in0=SPF.t[:, tt - 1, 32:64], in1=DEC.t[:, tt - 1, 4, 32:64], op=ALU.mult),
                             SPF.b + DEC.b, SPF.b)
                    k.op(dve, lambda h: h.tensor_tensor(out=DSEG.t[:, seg, 0:32], in0=SPF.t[:, 0, 0:32], in1=DEC.t[:, 0, 4, 0:32], op=ALU.mult),
                         SPF.b + DEC.b, [DSEG.b[seg]])
                    k.op(dve, lambda h: h.tensor_tensor(out=DSEG.t[:, seg, 32:64], in0=SPF.t[:, 3, 32:64], in1=DEC.t[:, 3, 4, 32:64], op=ALU.mult),
                         SPF.b + DEC.b, [DSEG.b[seg]])
                    k.op(dve, lambda h: h.tensor_tensor(out=MS.t[:], in0=DT.t[:, seg], in1=SPF.t[:], op=ALU.mult), [DT.b[seg]] + SPF.b, MS.b)
                    k.op(dve, lambda h: h.tensor_tensor(out=MS.t[:, :, 0:32], in0=MS.t[:, :, 0:32], in1=DEC.t[:, :, 1, 0:32], op=ALU.mult), MS.b + DEC.b, MS.b)
                    k.op(dve, lambda h: h.tensor_tensor(out=MS.t[:, :, 32:64], in0=MS.t[:, :, 32:64], in1=DEC.t[:, :, 3, 32:64], op=ALU.mult), MS.b + DEC.b, MS.b)
                    for g in range(4):
                        Wg = grp[cnt["g"] % len(grp)]
                        cnt["g"] += 1
                        li, ri = (seg - 1) % 4, (seg + 1) % 4
                        prep(seg, g, Wg, False, 1, (li, ri, mk(seg), mk(4 + seg)))
                        for dr in range(2):
                            sbk = bank()
                            hs = slice(dr * 32 + g * 8, dr * 32 + g * 8 + 8)
                            for tt in range(NT):
                                xi = cnt["x"] % 4
                                cnt["x"] += 1
                                k.op(dve, lambda h: h.tensor_tensor(out=R(XDDS.t[:, xi, :].rearrange("p (h d) -> p h d", d=64)),
                                                                    in0=Wg["XTOK"].t[:, tt, :].rearrange("p (h d) -> p h d", d=64),
                                                                    in1=bc8(MS.t[:, tt, hs]), op=ALU.mult),
                                     [Wg["XTOK"].b[tt]] + MS.b, [XDDS.b[xi]])
                                mm(sbk.t[:], Wg["BTOK"].t[:, tt, :], XDDS.t[:, xi, :], tt == 0, tt == NT - 1,
                                   [Wg["BTOK"].b[0], XDDS.b[xi]], sbk.b, inc=True)
                            if dr == 0:
                                step(HF, 0, g, si, si == 0, DSEG.t[:, seg, hs], [DSEG.b[seg]], sbk.t[:], sbk.b)
                            else:
                                k.op(act, lambda h: h.activation(out=SSB.t[:, seg - 1, g, :], in_=sbk.t[:], func=AF.Identity),
                                     sbk.b, [SSB.b[(seg - 1) * 4 + g]])
                        gemv2_step()
                for si, seg in enumerate([3, 2, 1]):
                    for g in range(4):
                        hs = slice(32 + g * 8, 32 + g * 8 + 8)
                        step(HB, 1, g, si, si == 0, DSEG.t[:, seg, hs], [DSEG.b[seg]], SSB.t[:, seg - 1, g, :], [SSB.b[(seg - 1) * 4 + g]])
                for g in range(4):
                    step(HF, 0, g, 3, False, None, None, None, None)
                    step(HB, 1, g, 3, False, None, None, None, None)
                gemv2_finish()
                k.barrier()

        def back(blk, H, t_, is_sample, y_out):
            with ExitStack() as ph:
                ACC = k.sb(ph, [128, NCK, T], NCK)
                RSTD = k.sb(ph, [128, T])
                with ExitStack() as p2:
                    DEC = k.sb(p2, [128, NT, 5, 64])
                    YACC = k.sb(p2, [128, NT, T], NT)
                    SQ = k.sb(p2, [128, 1, T], 1)
                    STO = k.sb(p2, [128, 4, 128])
                    grp, chk = alloc_work(p2, True, 1, 2)
                    W = grp[0]
                    YN = W["XG"]
                    cc_ = {"c": 0}
                    if VERBOSE:
                        print("back: chunk buffer sets", len(chk), "sbuf left", nc.sbuf_bytes_remaining)

                    def nxt():
                        cc_["c"] += 1
                        return chk[cc_["c"] % len(chk)]
                    make_dec(blk, DEC)
                    nsq = 0
                    for g in range(4):
                        wps = [wload(wso[(4 * g + i) * 128:(4 * g + i + 1) * 128, :]) for i in range(4)]
                        halo = (3, 1, PAR.t[:, P_MSK:P_MSK + 1], PAR.t[:, P_MSK + 4:P_MSK + 5]) if is_sample else None
                        prep(blk, g, W, True, 1 if is_sample else 2, halo)
                        for tt in range(NT):
                            k.op(dve, lambda h: h.tensor_tensor(
                                out=YACC.t[:, tt, :].rearrange("p (h d) -> p h d", d=64),
                                in0=W["XTOK"].t[:, tt, :].rearrange("p (h d) -> p h d", d=64),
                                in1=PAR.t[:, P_DSK + 8 * g:P_DSK + 8 * g + 8].unsqueeze(2).to_broadcast([128, 8, 64]), op=ALU.mult),
                                [W["XTOK"].b[tt], pB], [YACC.b[tt]])
                        items = []
                        for dr in range(2):
                            if is_sample:
                                HH = HF if dr == 0 else HB
                                for tt in (range(NT) if dr == 0 else range(NT - 1, -1, -1)):
                                    items.append((tt, dr, (HH.t[:, g, :], HH.b[g]), False, None))
                            else:
                                for sq in range(2):
                                    tts = [2 * sq, 2 * sq + 1] if dr == 0 else [2 * sq + 1, 2 * sq]

                                    def emit_state(dr=dr, sq=sq):
                                        bk = bank()
                                        for i in range(4):
                                            k.op(pe, lambda h: h.transpose(bk.t[:, i * 128:(i + 1) * 128], HF.t[:, dr, i * 128:(i + 1) * 128], ident),
                                                 [HF.b[dr], cB], bk.b, inc=(i == 3))
                                        k.op(act, lambda h: h.activation(out=STO.t[:].rearrange("p a b -> p (a b)"), in_=bk.t[:], func=AF.Identity),
                                             bk.b, STO.b)
                                        dst = (st_f if dr == 0 else st_b)[sq, g * 512:(g + 1) * 512, :].rearrange("(a p) n -> p a n", p=128)
                                        k.spdma(lambda h: h.dma_start(out=dst, in_=STO.t[:]), STO.b, [Buf()])
                                    for n_, tt in enumerate(tts):
                                        items.append((tt, dr, (HF.t[:, dr, :], HF.b[dr]), n_ == 0, emit_state if n_ == 1 else None))
                        ssd_run(blk, g, W, chk, DEC, YACC, items)
                        PRE = W["PRE"]
                        for i in range(4):
                            zt, zb_ = PRE.t[:, i % 2, 0:T], PRE.b[i % 2]
                            k.spdma(lambda h: h.dma_start(out=zt, in_=scr[blk, 24 + 4 * g + i]), [scrB[blk][24 + 4 * g + i]], [zb_])
                            bk = bank()
                            for tt in range(NT):
                                k.op(pe, lambda h: h.transpose(bk.t[:, tt * 128:(tt + 1) * 128], YACC.t[:, tt, i * 128:(i + 1) * 128], ident),
                                     [YACC.b[tt], cB], bk.b, inc=(tt == NT - 1))
                            k.op(act, lambda h: h.activation(out=zt, in_=zt, func=AF.Silu), [zb_], [zb_])
                            k.op(dve, lambda h: h.tensor_tensor(out=zt, in0=zt, in1=bk.t[:], op=ALU.mult), [zb_, bk.b[0]], [zb_])
                            sqb = SQ.b[0]
                            k.op(act, lambda h: h.activation(out=R(SQ.t[:, 0, :]), in_=zt, func=AF.Square), [zb_], [sqb])
                            mm(stat_bank.t[:], ones, SQ.t[:, 0, :], nsq == 0, nsq == 15, [cB, sqb], stat_bank.b, inc=True)
                            nsq += 1
                            c = 4 * g + i
                            k.op(dve, lambda h: h.tensor_scalar(out=R(YN.t[:, i, :]), in0=zt, scalar1=PAR.t[:, P_SN + c:P_SN + c + 1],
                                                                scalar2=None, op0=ALU.mult), [zb_, pB], [YN.b[i]])
                        for d_ in range(NCK):
                            ob = bank()
                            for i in range(4):
                                mm(ob.t[:], wps[i][0][:, d_ * 128:(d_ + 1) * 128], YN.t[:, i, :], i == 0, i == 3, [wps[i][1], YN.b[i]], ob.b)
                            if g == 0:
                                k.op(act, lambda h: h.activation(out=R(ACC.t[:, d_, :]), in_=ob.t[:], func=AF.Identity), ob.b, [ACC.b[d_]])
                            else:
                                k.op(dve, lambda h: h.tensor_tensor(out=R(ACC.t[:, d_, :]), in0=ACC.t[:, d_, :], in1=ob.t[:], op=ALU.add),
                                     [ACC.b[d_], ob.b[0]], [ACC.b[d_]])
                    k.op(dve, lambda h: h.tensor_scalar(out=RSTD.t[:], in0=stat_bank.t[:], scalar1=1.0 / D, scalar2=EPS,
                                                        op0=ALU.mult, op1=ALU.add), stat_bank.b, RSTD.b)
                    k.op(act, lambda h: h.activation(out=RSTD.t[:], in_=RSTD.t[:], func=AF.Sqrt), RSTD.b, RSTD.b)
                    k.op(dve, lambda h: h.reciprocal(out=RSTD.t[:], in_=RSTD.t[:]), RSTD.b, RSTD.b)
                    k.barrier()
                with ExitStack() as p2:
                    U = k.sb(p2, [128, NCK, T], NCK)
                    norm_mod(H, U, t_, 1, p2)
                    with ExitStack() as p3:
                        TM = k.sb(p3, [128, 4, T], 4)
                        NH = 1 if nc.sbuf_bytes_remaining >= 16 * T * 4 else 2
                        CPH = NCK // NH
                        if VERBOSE:
                            print('conv passes', NH)
                        PP = k.sb(p3, [128, CPH, T], CPH)
                        for d_ in range(NCK):
                            wv, wb = wload(colpanel(w_in, C_GS + d_ * 128), "p (k c) -> p k c", c=128)
                            bk = bank()
                            for kc in range(NCK):
                                mm(bk.t[:], wv[:, kc, :], U.t[:, kc, :], kc == 0, kc == NCK - 1, [wb, U.b[kc]], bk.b)
                            s = d_ % 2
                            k.op(act, lambda h: h.activation(out=TM.t[:, s, :], in_=bk.t[:], func=AF.Sigmoid), bk.b, [TM.b[s]])
                            k.op(dve, lambda h: h.tensor_tensor(out=TM.t[:, s, :], in0=TM.t[:, s, :], in1=RSTD.t[:], op=ALU.mult),
                                 [TM.b[s], RSTD.b[0]], [TM.b[s]])
                            k.op(dve, lambda h: h.tensor_tensor(out=R(ACC.t[:, d_, :]), in0=ACC.t[:, d_, :], in1=TM.t[:, s, :], op=ALU.mult),
                                 [ACC.b[d_], TM.b[s]], [ACC.b[d_]])
                        for half in range(NH):
                            Lr = 64 if is_sample else 256
                            for c in range(CPH * half, CPH * half + CPH):
                                bks = []
                                for col in (C_CC, C_CX, C_CB):
                                    wv, wb = wload(colpanel(w_in, col + c * 128), "p (k c) -> p k c", c=128)
                                    bk = bank()
                                    for kc in range(NCK):
                                        mm(bk.t[:], wv[:, kc, :], U.t[:, kc, :], kc == 0, kc == NCK - 1, [wb, U.b[kc]], bk.b)
                                    bks.append(bk)
                                pr, vv = TM.t[:, 2, :], TM.t[:, 3, :]
                                k.op(act, lambda h: h.activation(out=pr, in_=bks[0].t[:], func=AF.Identity), bks[0].b, [TM.b[2]])
                                k.op(dve, lambda h: h.tensor_tensor(out=pr, in0=pr, in1=bks[1].t[:], op=ALU.mult), [TM.b[2], bks[1].b[0]], [TM.b[2]])
                                w0, w1, w2 = (PAR.t[:, P_CW + 3 * c + i:P_CW + 3 * c + i + 1] for i in range(3))
                                p3 = pr.rearrange("p (s l) -> p s l", l=Lr)
                                v3 = vv.rearrange("p (s l) -> p s l", l=Lr)
                                k.op(dve, lambda h: h.tensor_scalar(out=vv, in0=pr, scalar1=w1, scalar2=None, op0=ALU.mult), [TM.b[2], pB], [TM.b[3]])
                                k.op(dve, lambda h: h.scalar_tensor_tensor(out=v3[:, :, 1:Lr], in0=p3[:, :, 0:Lr - 1], scalar=w0, in1=v3[:, :, 1:Lr],
                                                                           op0=ALU.mult, op1=ALU.add), [TM.b[2], TM.b[3], pB], [TM.b[3]])
                                k.op(dve, lambda h: h.scalar_tensor_tensor(out=v3[:, :, 0:Lr - 1], in0=p3[:, :, 1:Lr], scalar=w2, in1=v3[:, :, 0:Lr - 1],
                                                                           op0=ALU.mult, op1=ALU.add), [TM.b[2], TM.b[3], pB], [TM.b[3]])
                                k.op(dve, lambda h: h.tensor_tensor(out=R(PP.t[:, c % CPH, :]), in0=vv, in1=bks[2].t[:], op=ALU.mult),
                                     [TM.b[3], bks[2].b[0]], [PP.b[c % CPH]])
                            for d_ in range(NCK):
                                wv, wb = wload(colpanel(wco, d_ * 128), "p (k c) -> p k c", c=128)
                                gv, gb = wload(colpanel(w_in, C_GC + d_ * 128), "p (k c) -> p k c", c=128)
                                yb, gk = bank(), bank()
                                for kc in range(CPH):
                                    mm(yb.t[:], wv[:, CPH * half + kc, :], PP.t[:, kc, :], kc == 0, kc == CPH - 1, [wb, PP.b[kc]], yb.b)
                                for kc in range(NCK):
                                    mm(gk.t[:], gv[:, kc, :], U.t[:, kc, :], kc == 0, kc == NCK - 1, [gb, U.b[kc]], gk.b)
                                s = d_ % 2
                                k.op(act, lambda h: h.activation(out=TM.t[:, s, :], in_=gk.t[:], func=AF.Sigmoid), gk.b, [TM.b[s]])
                                k.op(dve, lambda h: h.tensor_tensor(out=TM.t[:, s, :], in0=TM.t[:, s, :], in1=yb.t[:], op=ALU.mult),
                                     [TM.b[s], yb.b[0]], [TM.b[s]])
                                k.op(dve, lambda h: h.tensor_tensor(out=R(ACC.t[:, d_, :]), in0=ACC.t[:, d_, :], in1=TM.t[:, s, :], op=ALU.add),
                                     [ACC.b[d_], TM.b[s]], [ACC.b[d_]])

                        k.barrier()
                    for d_ in range(NCK):
                        wv, wb = wload(colpanel(wo, d_ * 128), "p (k c) -> p k c", c=128)
                        ob = bank()
                        for kc in range(NCK):
                            mm(ob.t[:], wv[:, kc, :], ACC.t[:, kc, :], kc == 0, kc == NCK - 1, [wb, ACC.b[kc]], ob.b)
                        k.op(dve, lambda h: h.scalar_tensor_tensor(out=H.t[:, d_, :], in0=ob.t[:], scalar=mcol(t_, 5, d_), in1=H.t[:, d_, :],
                                                                   op0=ALU.mult, op1=ALU.add), [ob.b[0], MODD.b[0], H.b[d_]], [H.b[d_]])
                    norm_mod(H, U, t_, 2, p2)
                    ffn(H, U, t_, 2, f2g, f2u, f2d)
                    k.barrier()
                k.barrier()
            with ExitStack() as ph:
                FW = k.sb(ph, [128, D])
                YT = k.sb(ph, [128, 2, D], 2)
                JK = k.sb(ph, [128, D])
                SS = k.sb(ph, [128, 2], 2)
                k.spdma(lambda h: h.dma_start(out=FW.t[:], in_=fnwd.partition_broadcast(128)), [], FW.b)
                for tt in range(NT):
                    s = tt % 2
                    for q4 in range(4):
                        bk = bank()
                        for i in range(4):
                            c = 4 * q4 + i
                            k.op(pe, lambda h: h.transpose(bk.t[:, i * 128:(i + 1) * 128], H.t[:, c, tt * 128:(tt + 1) * 128], ident),
                                 [H.b[c], cB], bk.b, inc=(i == 3))
                        k.op(act, lambda h: h.activation(out=YT.t[:, s, q4 * 512:(q4 + 1) * 512], in_=bk.t[:], func=AF.Identity),
                             bk.b, [YT.b[s]])
                    k.op(dve, lambda h: h.tensor_tensor(out=JK.t[:], in0=YT.t[:, s, :], in1=YT.t[:, s, :], op=ALU.mult), [YT.b[s]], JK.b)
                    k.op(dve, lambda h: h.reduce_sum(out=SS.t[:, s:s + 1], in_=JK.t[:], axis=mybir.AxisListType.X), JK.b, [SS.b[s]])
                    k.op(dve, lambda h: h.tensor_scalar(out=SS.t[:, s:s + 1], in0=SS.t[:, s:s + 1], scalar1=1.0 / D, scalar2=EPS,
                                                        op0=ALU.mult, op1=ALU.add), [SS.b[s]], [SS.b[s]])
                    k.op(act, lambda h: h.activation(out=SS.t[:, s:s + 1], in_=SS.t[:, s:s + 1], func=AF.Sqrt), [SS.b[s]], [SS.b[s]])
                    k.op(dve, lambda h: h.reciprocal(out=SS.t[:, s:s + 1], in_=SS.t[:, s:s + 1]), [SS.b[s]], [SS.b[s]])
                    k.op(dve, lambda h: h.scalar_tensor_tensor(out=YT.t[:, s, :], in0=YT.t[:, s, :], scalar=SS.t[:, s:s + 1], in1=FW.t[:],
                                                               op0=ALU.mult, op1=ALU.mult), [YT.b[s], SS.b[s], FW.b[0]], [YT.b[s]])
                    k.spdma(lambda h: h.dma_start(out=y_out[tt * 128:(tt + 1) * 128, :], in_=YT.t[:, s, :]), [YT.b[s]], [Buf()])
                k.barrier()

        if stage >= 1:
            front(0, H0, 1, True)
            tap("h1_s", H0.t[:].rearrange("p a b -> p (a b)"), H0.b)
            tap("dt_s", DT.t[:, 0].rearrange("p a b -> p (a b)"), DT.b)
        if stage >= 2:
            with ExitStack() as ph:
                HQ = k.sb(ph, [128, NCK, T], NCK)
                for q in (1, 2, 3):
                    front(q, HQ, 1, False)
                k.barrier()
        if stage >= 3:
            state_pass()
            tap("hf", HF.t[:].rearrange("p a b -> p (a b)"), HF.b)
            tap("hb", HB.t[:].rearrange("p a b -> p (a b)"), HB.b)
        if stage >= 4:
            back(0, H0, 1, True, y_s)
        if stage >= 5:
            front(4, H0, 0, True)
            back(4, H0, 0, False, y_p)
        e = k.sp
        for s in k.lanes + [x.sig for x in k.engs if x is not e]:
            if s.count > e.seen.get(s, 0):
                e.h.wait_ge(s.sem, s.count)
                e.seen[s] = s.count
    return nc, dbg_out


def _fm(v, n):
    return np.ascontiguousarray(np.asarray(v, np.float32).reshape(n, 128).T)


def _consts():
    i = np.arange(128)
    k_, s_ = i[:, None], i[None, :]
    mats = [np.eye(128), np.ones((128, 128)), k_ <= s_, k_ > s_, k_ >= s_, k_ < s_]
    return np.ascontiguousarray(np.stack([m.astype(np.float32) for m in mats], 1).reshape(128, 6 * 128))


def make_inputs(core, I, shared):
    b, q = core // 4, core % 4
    xsam = I["x_sample"][b]
    order = [(q + i) % 4 for i in range(4)]
    blocks = [xsam[512 * s:512 * (s + 1)] for s in order]
    blocks.append(I["x_prompt"][2 * core:2 * core + 2].reshape(512, D))
    xs = np.ascontiguousarray(np.stack(blocks, 0), dtype=np.float32)
    cvv = np.stack([I["c_ctx"], I["c"][b]], 0).astype(np.float32)
    cv = np.ascontiguousarray(cvv.reshape(2, 16, 128).transpose(2, 1, 0).reshape(128, 32))
    h0 = np.ascontiguousarray(np.stack([I["state_ssm_fwd"][b, 0].reshape(2048, 128),
                                        I["state_ssm_bwd"][b, 0].reshape(2048, 128)], 0), dtype=np.float32)
    par = np.zeros((128, NPAR), np.float32)
    par[:, P_BADA:P_BADA + 144] = _fm(I["b_ada"][0], 144)
    par[:, P_N1:P_N1 + 16] = _fm(I["norm1_w"][0], 16)
    par[:, P_N2:P_N2 + 16] = _fm(I["norm2_w"][0], 16)
    par[:, P_N3:P_N3 + 16] = _fm(I["norm3_w"][0], 16)
    par[:, P_SN:P_SN + 16] = _fm(I["ssm_norm_w"][0], 16)
    cw = np.stack([_fm(I["conv_w"][0, i], 16) for i in range(3)], 2)
    par[:, P_CW:P_CW + 48] = cw.reshape(128, 48)
    scw = np.stack([_fm(I["ssm_conv_w"][0, i], 24) for i in range(3)], 2)
    par[:, P_SCW:P_SCW + 72] = scw.reshape(128, 72)
    par[:, P_SCB:P_SCB + 24] = _fm(I["ssm_conv_b"][0], 24)
    par[:, P_DTB:P_DTB + 64] = np.concatenate([I["dt_bias_f"][0], I["dt_bias_b"][0]])[None, :]
    par[:, P_ALOG:P_ALOG + 64] = np.concatenate([I["a_log_f"][0], I["a_log_b"][0]])[None, :]
    par[:, P_DSK:P_DSK + 32] = np.asarray(I["d_skip"][0])[None, :]
    m = np.zeros(24, np.float32)
    for i in range(4):
        true = order[i]
        m[i] = 0.0 if true == 0 else 1.0
        m[4 + i] = 0.0 if true == 3 else 1.0
    for si, seg in enumerate([1, 2, 3]):
        w = 1.0 if order[seg] == 0 else 0.0
        m[8 + si], m[12 + si] = w, 1.0 - w
    w = 1.0 if q == 0 else 0.0
    m[8 + 3], m[12 + 3] = w, 1.0 - w
    for si, seg in enumerate([3, 2, 1]):
        w = 1.0 if order[seg] == 3 else 0.0
        m[16 + si], m[20 + si] = w, 1.0 - w
    w = 1.0 if q == 3 else 0.0
    m[16 + 3], m[20 + 3] = w, 1.0 - w
    par[:, P_MSK:P_MSK + 24] = m[None, :]
    d = {"xs": xs, "cv": cv, "h0": h0, "par": par, "cst": _consts()}
    d.update(shared)
    return d


def _tile_cols(w):
    w = np.asarray(w, np.float32)
    K, C = w.shape
    return np.ascontiguousarray(w.reshape(K // 128, 128, C // 128, 128).transpose(2, 1, 0, 3).reshape(C // 128, 128, K))


def make_shared(I):
    f32 = lambda a: np.ascontiguousarray(a, dtype=np.float32)
    win = np.asarray(I["w_in"][0], np.float32)
    w_in_t = np.concatenate([_tile_cols(win[:, :C_DT]), _tile_cols(win[:, C_GC:])], 0)
    w_dt = np.ascontiguousarray(win[:, C_DT:C_GC].reshape(16, 128, 64).transpose(1, 0, 2).reshape(128, 1024))
    return {
        "fnw": f32(I["final_norm_w"]),
        "w_ada": _tile_cols(I["w_ada"][0]), "f1g": _tile_cols(I["ffn1_w_gate"][0]), "f1u": _tile_cols(I["ffn1_w_up"][0]),
        "f1d": f32(I["ffn1_w_down"][0]), "f2g": _tile_cols(I["ffn2_w_gate"][0]), "f2u": _tile_cols(I["ffn2_w_up"][0]),
        "f2d": f32(I["ffn2_w_down"][0]), "w_in": w_in_t, "w_dt": w_dt, "wco": _tile_cols(I["w_conv_out"][0]),
        "wso": f32(I["w_ssm_out"][0]), "wo": _tile_cols(I["w_o"][0]),
    }


def kernel(**inputs):
    I = {k_: np.asarray(v) for k_, v in inputs.items()}
    nc, _ = build()
    shared = make_shared(I)
    in_maps = [make_inputs(c, I, shared) for c in range(8)]
    res = run_bass_kernel_spmd(nc, in_maps, core_ids=list(range(8)))
    R_ = res.results
    y_prompt = np.zeros((16, 256, D), np.float32)
    y_sample = np.zeros((2, 2048, D), np.float32)
    nf = np.zeros((16, 1, 32, 64, 128), np.float32)
    nb = np.zeros((16, 1, 32, 64, 128), np.float32)
    for c in range(8):
        b, q = c // 4, c % 4
        y_prompt[2 * c:2 * c + 2] = np.asarray(R_[c]["y_p"]).reshape(2, 256, D)
        y_sample[b, 512 * q:512 * (q + 1)] = np.asarray(R_[c]["y_s"])
        nf[2 * c:2 * c + 2, 0] = np.asarray(R_[c]["st_f"]).reshape(2, 32, 64, 128)
        nb[2 * c:2 * c + 2, 0] = np.asarray(R_[c]["st_b"]).reshape(2, 32, 64, 128)
    return (y_prompt, y_sample, nf, nb)
```
